# Optimizing a Trainium2 kernel written in Bass

```python
import math
import jax, jax.numpy as jnp
from jax import lax
import numpy as np

D_MODEL = 4096
BATCH = 4
SEQ = 4096
DEPTH = 1

RMS_EPS = 1e-5
SSM_WIDTH = D_MODEL // 2
SSM_GROUP = 16
SSM_GROUPS = SSM_WIDTH // SSM_GROUP
SSM_STATE = 64
DT_MIN = 1e-3
DT_MAX = 1e-1
HEAD_DIM = 128
ATTN_HEADS = D_MODEL // 512
DILATION_PATTERNS = ((128, 1), (512, 4), (2048, 16))
N_ATTN_GROUPS = len(DILATION_PATTERNS)
ATTN_GROUP_WIDTH = ATTN_HEADS * HEAD_DIM
ATTN_QKV_WIDTH = N_ATTN_GROUPS * ATTN_GROUP_WIDTH
ROPE_THETA = 500000.0
ROPE_DIM = HEAD_DIM // 4
D_FF = 256 * ((8 * D_MODEL // 3 + 255) // 256)
CONV_WIDTH = 3
IN_SPLITS = (SSM_WIDTH, ATTN_QKV_WIDTH, ATTN_QKV_WIDTH, ATTN_QKV_WIDTH, D_MODEL, D_MODEL)
IN_COLS = sum(IN_SPLITS)

kernel_name = "hybrid_s5_dilated_attn_gated_block"


def rms_norm(x, g):
    xf = x.astype(jnp.float32)
    y = xf * lax.rsqrt(jnp.mean(xf * xf, axis=-1, keepdims=True) + RMS_EPS)
    return (y * g.astype(jnp.float32)).astype(x.dtype)


def partial_rotary(t):
    seq = t.shape[1]
    half = ROPE_DIM // 2
    inv_freq = ROPE_THETA ** (-jnp.arange(0, ROPE_DIM, 2, dtype=jnp.float32) / ROPE_DIM)
    ang = jnp.arange(seq, dtype=jnp.float32)[:, None] * inv_freq[None, :]
    cos = jnp.cos(ang)[None, :, None, None, :]
    sin = jnp.sin(ang)[None, :, None, None, :]
    tf = t.astype(jnp.float32)
    t1, t2, rest = tf[..., :half], tf[..., half:ROPE_DIM], tf[..., ROPE_DIM:]
    out = jnp.concatenate([t1 * cos - t2 * sin, t2 * cos + t1 * sin, rest], axis=-1)
    return out.astype(t.dtype)


def dilated_window_attention(q, k, v, dilation, n_back):
    b, l, h, dh = q.shape
    n = l // dilation
    blk = n_back
    nb = -(-n // blk)
    n_pad = nb * blk

    def by_residue(t):
        return jnp.transpose(t.reshape(b, n, dilation, h, dh), (0, 2, 3, 1, 4))

    qr, kr, vr = by_residue(q), by_residue(k), by_residue(v)
    qb = jnp.pad(qr, ((0, 0), (0, 0), (0, 0), (0, n_pad - n), (0, 0)))
    qb = qb.reshape(b, dilation, h, nb, blk, dh)

    def key_blocks(t):
        tp = jnp.pad(t, ((0, 0), (0, 0), (0, 0), (blk, n_pad - n), (0, 0)))
        prev = tp[..., :n_pad, :].reshape(b, dilation, h, nb, blk, dh)
        cur = tp[..., blk:, :].reshape(b, dilation, h, nb, blk, dh)
        return jnp.concatenate([prev, cur], axis=-2)

    kb, vb = key_blocks(kr), key_blocks(vr)
    s = jnp.einsum('bdhnqc,bdhnkc->bdhnqk', qb, kb,
                   preferred_element_type=jnp.float32) * (dh ** -0.5)
    qi = jnp.arange(blk)[:, None]
    kj = jnp.arange(2 * blk)[None, :]
    dist = blk + qi - kj
    band = (dist >= 0) & (dist <= n_back)
    has_key = (jnp.arange(nb)[:, None, None] > 0) | (kj >= blk)[None]
    mask = band[None] & has_key
    s = jnp.where(mask, s, -jnp.inf)
    lse = jax.nn.logsumexp(s, axis=-1)
    p = jnp.exp(s - lse[..., None])
    o = jnp.einsum('bdhnqk,bdhnkc->bdhnqc', p, vb.astype(jnp.float32))
    o = o.reshape(b, dilation, h, n_pad, dh)[..., :n, :]
    lse = lse.reshape(b, dilation, h, n_pad)[..., :n]
    o = jnp.transpose(o, (0, 3, 1, 2, 4)).reshape(b, l, h, dh)
    lse = jnp.transpose(lse, (0, 3, 1, 2)).reshape(b, l, h)
    return o, lse


def complex_linear_combine(e1, e2):
    a1r, a1i, b1r, b1i = e1
    a2r, a2i, b2r, b2i = e2
    ar = a2r * a1r - a2i * a1i
    ai = a2r * a1i + a2i * a1r
    br = a2r * b1r - a2i * b1i + b2r
    bi = a2r * b1i + a2i * b1r + b2i
    return (ar, ai, br, bi)


def s5_mixer(u, log_dt, a_re, a_im, b_re, b_im, c_re, c_im, d_skip):
    f32 = jnp.float32
    uf = u.astype(f32)
    a_re, a_im = a_re.astype(f32), a_im.astype(f32)
    dt = jnp.exp(log_dt.astype(f32))[:, None]
    mag = jnp.exp(dt * a_re)
    lb_re, lb_im = mag * jnp.cos(dt * a_im), mag * jnp.sin(dt * a_im)
    den = a_re * a_re + a_im * a_im
    f_re = ((lb_re - 1.0) * a_re + lb_im * a_im) / den
    f_im = (lb_im * a_re - (lb_re - 1.0) * a_im) / den
    br, bi = b_re.astype(f32), b_im.astype(f32)
    bb_re = f_re[..., None] * br - f_im[..., None] * bi
    bb_im = f_re[..., None] * bi + f_im[..., None] * br
    cr, ci = c_re.astype(f32), c_im.astype(f32)
    dsk = d_skip.astype(f32)

    def one_sequence(us):
        bu_re = jnp.einsum('gpc,lgc->lgp', bb_re, us)
        bu_im = jnp.einsum('gpc,lgc->lgp', bb_im, us)
        ar = jnp.broadcast_to(lb_re, bu_re.shape)
        ai = jnp.broadcast_to(lb_im, bu_re.shape)
        _, _, x_re, x_im = lax.associative_scan(
            complex_linear_combine, (ar, ai, bu_re, bu_im), axis=0)
        y = (jnp.einsum('gcp,lgp->lgc', cr, x_re)
             - jnp.einsum('gcp,lgp->lgc', ci, x_im))
        return y + dsk * us

    return lax.map(one_sequence, uf)


def setup_inputs(seed: int = 0) -> dict:
    key = jax.random.key(seed)
    ks = jax.random.split(key, 20)
    f32 = jnp.float32

    def nrm(k, shape, fan_in):
        return jax.random.normal(k, shape, f32) * (fan_in ** -0.5)

    G, P, C = SSM_GROUPS, SSM_STATE, SSM_GROUP
    x = jax.random.normal(ks[0], (BATCH, SEQ, D_MODEL), f32)
    g_mix = 1.0 + 0.02 * jax.random.normal(ks[1], (DEPTH, D_MODEL), f32)
    w_in = nrm(ks[2], (DEPTH, D_MODEL, IN_COLS), D_MODEL)
    ssm_log_dt = jax.random.uniform(ks[3], (DEPTH, G), f32,
                                    minval=math.log(DT_MIN), maxval=math.log(DT_MAX))
    ssm_a_re = -0.5 + 0.01 * jax.random.normal(ks[4], (DEPTH, G, P), f32)
    ssm_a_im = (math.pi * jnp.arange(P, dtype=f32))[None, None, :] \
        + 0.01 * jax.random.normal(ks[5], (DEPTH, G, P), f32)
    ssm_b_re = nrm(ks[6], (DEPTH, G, P, C), 2 * C)
    ssm_b_im = nrm(ks[7], (DEPTH, G, P, C), 2 * C)
    ssm_c_re = nrm(ks[8], (DEPTH, G, C, P), 2 * P)
    ssm_c_im = nrm(ks[9], (DEPTH, G, C, P), 2 * P)
    ssm_d = jax.random.normal(ks[10], (DEPTH, G, C), f32)
    w_glu = nrm(ks[11], (DEPTH, SSM_WIDTH, 2 * D_MODEL), SSM_WIDTH)
    w_attn_out = nrm(ks[12], (DEPTH, ATTN_GROUP_WIDTH, D_MODEL), ATTN_GROUP_WIDTH)
    w_out = nrm(ks[13], (DEPTH, D_MODEL, D_MODEL), D_MODEL)
    g_ffn = 1.0 + 0.02 * jax.random.normal(ks[14], (DEPTH, D_MODEL), f32)
    w_up = nrm(ks[15], (DEPTH, D_MODEL, 2 * D_FF), D_MODEL)
    conv_w = nrm(ks[16], (DEPTH, CONV_WIDTH, 2 * D_FF), CONV_WIDTH)
    conv_b = 0.01 * jax.random.normal(ks[17], (DEPTH, 2 * D_FF), f32)
    w_down = nrm(ks[18], (DEPTH, D_FF, D_MODEL), D_FF)
    g_final = 1.0 + 0.02 * jax.random.normal(ks[19], (D_MODEL,), f32)
    return {"x": x, "g_mix": g_mix, "w_in": w_in, "ssm_log_dt": ssm_log_dt,
            "ssm_a_re": ssm_a_re, "ssm_a_im": ssm_a_im, "ssm_b_re": ssm_b_re,
            "ssm_b_im": ssm_b_im, "ssm_c_re": ssm_c_re, "ssm_c_im": ssm_c_im,
            "ssm_d": ssm_d, "w_glu": w_glu, "w_attn_out": w_attn_out, "w_out": w_out,
            "g_ffn": g_ffn, "w_up": w_up, "conv_w": conv_w, "conv_b": conv_b,
            "w_down": w_down, "g_final": g_final}


def reference(x, g_mix, w_in, ssm_log_dt, ssm_a_re, ssm_a_im, ssm_b_re, ssm_b_im,
              ssm_c_re, ssm_c_im, ssm_d, w_glu, w_attn_out, w_out, g_ffn, w_up,
              conv_w, conv_b, w_down, g_final):
    b, l, _ = x.shape
    split_points = list(np.cumsum(IN_SPLITS)[:-1])
    for i in range(DEPTH):
        h = rms_norm(x, g_mix[i])
        proj = h @ w_in[i]
        u, q, k, v, gate_s, gate_a = jnp.split(proj, split_points, axis=-1)

        u = u.reshape(b, l, SSM_GROUPS, SSM_GROUP)
        y = s5_mixer(u, ssm_log_dt[i], ssm_a_re[i], ssm_a_im[i], ssm_b_re[i], ssm_b_im[i],
                     ssm_c_re[i], ssm_c_im[i], ssm_d[i])
        y = jax.nn.gelu(y.reshape(b, l, SSM_WIDTH)).astype(x.dtype)
        glu_a, glu_b = jnp.split(y @ w_glu[i], 2, axis=-1)
        ssm_branch = glu_a * jax.nn.sigmoid(glu_b)

        shp = (b, l, N_ATTN_GROUPS, ATTN_HEADS, HEAD_DIM)
        q = partial_rotary(q.reshape(shp))
        k = partial_rotary(k.reshape(shp))
        v = v.reshape(shp)
        outs, lses = [], []
        for g, (window, dilation) in enumerate(DILATION_PATTERNS):
            o_g, lse_g = dilated_window_attention(q[:, :, g], k[:, :, g], v[:, :, g],
                                                  dilation, window // dilation)
            outs.append(o_g)
            lses.append(lse_g)
        wts = jax.nn.softmax(jnp.stack(lses, axis=0), axis=0)
        attn = jnp.sum(wts[..., None] * jnp.stack(outs, axis=0), axis=0)
        attn = attn.reshape(b, l, ATTN_GROUP_WIDTH).astype(x.dtype)
        attn_branch = attn @ w_attn_out[i]

        mixed = jax.nn.sigmoid(gate_s) * ssm_branch + jax.nn.sigmoid(gate_a) * attn_branch
        x = x + mixed @ w_out[i]

        h = rms_norm(x, g_ffn[i])
        up = h @ w_up[i]
        up = lax.conv_general_dilated(
            up, conv_w[i][:, None, :].astype(up.dtype), window_strides=(1,),
            padding=[(CONV_WIDTH - 1, 0)], dimension_numbers=('NWC', 'WIO', 'NWC'),
            feature_group_count=2 * D_FF) + conv_b[i]
        a, gv = jnp.split(up, 2, axis=-1)
        x = x + (jax.nn.silu(a) * gv) @ w_down[i]
    return rms_norm(x, g_final)
```

```python
import math
from contextlib import ExitStack

import numpy as np
import ml_dtypes

import concourse.bass as bass
import concourse.mybir as mybir
from concourse.bass_utils import run_bass_kernel_spmd

F32 = mybir.dt.float32
BF16 = mybir.dt.bfloat16
AF = mybir.ActivationFunctionType
ALU = mybir.AluOpType

D = 4096
SEQ = 4096
NCORES = 8
NTE = 4096
OWN0 = 2048
EXT0 = 1920
NEXT = NTE - EXT0
DFF = 11008
NFB = DFF // 128
IN_COLS = 19456
RMS_EPS = 1e-5
NEG = -30000.0
SAME_ENG_SYNC = True
DMA_RING = 6
W_BPP = {"w_in": 8, "w_glu": 4, "w_ao": 8, "w_out": 4, "w_up": 4, "w_down": 1}


class Buf:
    __slots__ = ("name", "w", "r")

    def __init__(self, name=""):
        self.name = name
        self.w = None
        self.r = {}


class DBuf(Buf):
    pass


class T:
    __slots__ = ("h", "b")

    def __init__(self, h, name=""):
        self.h = h
        self.b = Buf(name)

    def __getitem__(self, k):
        return self.h[k]


class Sched:
    def __init__(self, nc):
        self.nc = nc
        self.eng = {}
        for n in ("tensor", "vector", "scalar", "gpsimd", "sync"):
            self.eng[n] = dict(e=getattr(nc, n), sem=nc.alloc_semaphore("sem_" + n), cnt=0, known={})
        self.dq = {}
        for q in ("sync", "gpsimd", "scalar"):
            self.dq[q] = dict(sems=[nc.alloc_semaphore("dq_%s_%d" % (q, i)) for i in range(DMA_RING)],
                              vals=[0] * DMA_RING, i=0)

    def _deps(self, reads, writes):
        deps = {}

        def add(d):
            if d is None:
                return
            k, sem, v = d
            if k not in deps or deps[k][1] < v:
                deps[k] = (sem, v)

        for b in reads:
            add(b.w)
        for b in writes:
            add(b.w)
            for d in b.r.values():
                add(d)
        return deps

    def _wait(self, en, deps):
        E = self.eng[en]
        for k, (sem, v) in deps.items():
            if k == en and (en == "tensor" or not SAME_ENG_SYNC):
                continue
            if E["known"].get(k, 0) >= v:
                continue
            E["e"].wait_ge(sem, v)
            E["known"][k] = v

    @staticmethod
    def _bufs(xs):
        return [x.b if isinstance(x, T) else x for x in xs if not isinstance(x, DBuf)]

    def op(self, en, fn, reads=(), writes=(), sig=True):
        reads = self._bufs(reads)
        writes = self._bufs(writes)
        E = self.eng[en]
        self._wait(en, self._deps(reads, writes))
        inst = fn()
        if sig:
            E["cnt"] += 1
            inst.then_inc(E["sem"], 1)
            tok = (en, E["sem"], E["cnt"])
        else:
            tok = (en, E["sem"], E["cnt"] + 1)
        for b in writes:
            b.w = tok
            b.r = {}
        for b in reads:
            b.r[en] = tok
        return inst

    def dma(self, q, out, in_, reads=(), writes=()):
        reads = self._bufs(reads)
        writes = self._bufs(writes)
        Q = self.dq[q]
        E = self.eng[q]
        self._wait(q, self._deps(reads, writes))
        slot = Q["i"] % DMA_RING
        Q["i"] += 1
        key = ("dma", q, slot)
        if Q["vals"][slot] > 0 and E["known"].get(key, 0) < Q["vals"][slot]:
            E["e"].wait_ge(Q["sems"][slot], Q["vals"][slot])
            E["known"][key] = Q["vals"][slot]
        inst = E["e"].dma_start(out=out, in_=in_)
        Q["vals"][slot] += 16
        inst.then_inc(Q["sems"][slot], 16)
        tok = (key, Q["sems"][slot], Q["vals"][slot])
        for b in writes:
            b.w = tok
            b.r = {}
        for b in reads:
            b.r[key] = tok
        return inst

    def barrier(self):
        toks = []
        for en, E in self.eng.items():
            if E["cnt"] > 0:
                toks.append((en, E["sem"], E["cnt"]))
        for q, Q in self.dq.items():
            for s in range(DMA_RING):
                if Q["vals"][s] > 0:
                    toks.append((("dma", q, s), Q["sems"][s], Q["vals"][s]))
        for en, E in self.eng.items():
            for k, sem, v in toks:
                if k == en:
                    continue
                if E["known"].get(k, 0) >= v:
                    continue
                E["e"].wait_ge(sem, v)
                E["known"][k] = v


_UID = [0]


class Ctx:
    def __init__(self, nc, S):
        self.nc = nc
        self.S = S
        self.es = ExitStack()
        self.n = 0

    def __enter__(self):
        self.es.__enter__()
        return self

    def __exit__(self, *a):
        self.S.barrier()
        return self.es.__exit__(*a)

    def sb(self, name, shape, dt):
        _UID[0] += 1
        return T(self.es.enter_context(self.nc.sbuf_tensor("%s_%d" % (name, _UID[0]), list(shape), dt)), name)

    def ps(self, name, shape, dt):
        _UID[0] += 1
        return T(self.es.enter_context(self.nc.psum_tensor("%s_%d" % (name, _UID[0]), list(shape), dt)), name)


class Rot:
    def __init__(self, items):
        self.items = items
        self.i = 0

    def next(self):
        x = self.items[self.i % len(self.items)]
        self.i += 1
        return x


def tok_groups(t0, n, g=512):
    out = []
    while n > 0:
        m = min(g, n)
        out.append((t0, m))
        t0 += m
        n -= m
    return out


def build_program(debug=False, nstages=99, s1mode=99, sel=None, ssm_mode=99):
    nc = bass.Bass("TRN2", target_bir_lowering=False)
    S = Sched(nc)
    V, A, G, PE = "vector", "scalar", "gpsimd", "tensor"

    def din(name, shape, dt=F32):
        return nc.dram_tensor(name, list(shape), dt, kind="ExternalInput").ap()

    def dscr(name, shape, dt):
        kind = "ExternalOutput" if debug else "Internal"
        return nc.dram_tensor(name, list(shape), dt, kind=kind).ap()

    class LazyW:
        def __init__(self, name, shape):
            self.name, self.shape, self.aps = name, shape, {}
            self.bpp = W_BPP[name]

        def __getitem__(self, k):
            if isinstance(k, tuple):
                blk, rest = k[0], k[1:]
            else:
                blk, rest = k, None
            pc = blk // self.bpp
            if pc not in self.aps:
                nb = min(self.bpp, self.shape[0] - pc * self.bpp)
                self.aps[pc] = din("%s_p%d" % (self.name, pc), [nb] + list(self.shape[1:]))
            ap = self.aps[pc][blk % self.bpp]
            return ap[rest] if rest is not None else ap

    x_d = din("x", [NTE, D])
    w_in = LazyW("w_in", [IN_COLS // 256, 128, 32 * 256])
    w_glu = LazyW("w_glu", [8, 128, 16 * 2 * 512])
    w_ao = LazyW("w_ao", [8, 128, 8 * 512])
    w_out = LazyW("w_out", [8, 128, 32 * 512])
    w_up = LazyW("w_up", [NFB // 2, 128, 32 * 2 * 256])
    w_down = LazyW("w_down", [8, 128, NFB * 512])
    g3_d = din("g3", [3, 128, D])
    convp_d = din("convp", [128, 4, 2 * NFB])
    ssm_tm_d = din("ssm_tm", [3, 128, 8192])
    ssm_sm_d = din("ssm_sm", [3, 128, 64])
    bblk_d = din("bblk", [128, 16, 1024])
    cblk_d = din("cblk", [2, 128, 64, 128])
    dsk_d = din("dsk", [128, 16])
    sidx_d = din("sidx", [128, 1])
    tidx_d = din("tidx", [128, 128])
    cst_d = din("cst", [6, 128, 128])
    ones_d = din("ones", [128, 128])
    pos_d = din("pos", [128, NTE])
    invf_d = din("invf", [128, 1])

    out_d = nc.dram_tensor("out", [2048, D], F32, kind="ExternalOutput").ap()

    h1T_d = dscr("h1T", [32, 128, NTE], BF16)
    uT_d = dscr("uT", [16, 128, NTE], BF16)
    qT_d = dscr("qT", [24, 128, NEXT], BF16)
    kT_d = dscr("kT", [24, 128, NTE], BF16)
    v3_d = dscr("vtok", [32, 128, 3072], BF16)
    v_d = v3_d.rearrange("a p c -> (a p) c")
    gT_d = dscr("gT", [64, 128, NEXT], BF16)
    yT_d = dscr("yT", [16, 128, NEXT], BF16)
    attnT_d = dscr("attnT", [8, 128, NEXT], BF16)
    mixT_d = dscr("mixT", [32, 128, NEXT], BF16)
    x1_3d = dscr("x1", [17, 128, D], F32)
    x1_d = x1_3d.rearrange("a p c -> (a p) c")
    h2T_d = dscr("h2T", [32, 128, NEXT], BF16)
    actT_d = dscr("actT", [NFB, 128, 2048], BF16)
    x2_3d = dscr("x2", [16, 128, D], F32)
    x2_d = x2_3d.rearrange("a p c -> (a p) c")

    DB = {n: DBuf(n) for n in ("h1T", "uT", "qT", "kT", "v", "gT", "yT", "attnT", "mixT", "x1", "h2T", "actT", "x2",
                              "out", "in")}

    top = ExitStack()

    def psb(name, shape, dt):
        return T(top.enter_context(nc.sbuf_tensor(name, list(shape), dt)), name)

    ident = psb("ident", [128, 128], BF16)
    S.dma(G, ident[:], cst_d[1], writes=[ident])
    _cm = {0: psb("cms0", [128, 64], F32), 1: psb("cms1", [128, 64], F32), "kk": psb("sc_kk", [128, 128], F32)}
    ssq = psb("ssq", [128, 16, 8], F32)

    def stage_normT(x_ap, chunks, gidx, hT_ap, tok_off, rd, wr):
        with Ctx(nc, S) as C:
            gbc = C.sb("gbc", [128, D], F32)
            S.dma("sync", gbc[:], g3_d[gidx], writes=[gbc])
            xt = Rot([C.sb("xt", [128, D], F32) for _ in range(2)])
            junk = C.sb("junk", [128, D], BF16)
            hb = Rot([C.sb("hb", [128, D], BF16) for _ in range(2)])
            hTs = Rot([C.sb("hTs", [128, 32, 512], BF16) for _ in range(2)])
            ss = Rot([C.sb("ss", [128, 1], F32) for _ in range(2)])
            PT = Rot([C.ps("PT", [128, 8, 128], BF16) for _ in range(4)])
            cp = 0
            for g0 in range(0, len(chunks), 4):
                grp = chunks[g0:g0 + 4]
                st = hTs.next()
                for ci, c in enumerate(grp):
                    X = xt.next()
                    r0 = c * 128 - tok_off
                    S.dma("sync", X[:], x_ap[r0:r0 + 128, :], reads=[rd], writes=[X])
                    s_ = ss.next()
                    S.op(A, lambda: nc.scalar.activation(out=junk[:], in_=X[:], func=AF.Square, accum_out=s_[:]),
                         reads=[X], writes=[junk, s_])
                    S.op(V, lambda: nc.vector.tensor_scalar(out=s_[:], in0=s_[:], scalar1=1.0 / D, scalar2=RMS_EPS,
                                                            op0=ALU.mult, op1=ALU.add), reads=[s_], writes=[s_])
                    S.op(A, lambda: nc.scalar.activation(out=s_[:], in_=s_[:], func=AF.Sqrt), reads=[s_], writes=[s_])
                    S.op(V, lambda: nc.vector.reciprocal(out=s_[:], in_=s_[:]), reads=[s_], writes=[s_])
                    H = hb.next()
                    S.op(V, lambda: nc.vector.scalar_tensor_tensor(out=H[:], in0=X[:], scalar=s_[:], in1=gbc[:],
                                                                   op0=ALU.mult, op1=ALU.mult),
                         reads=[X, s_, gbc], writes=[H])
                    for q in range(4):
                        P = PT.next()
                        for k8 in range(8):
                            k = q * 8 + k8
                            S.op(PE, lambda: nc.tensor.transpose(out=P[:, k8, :], in_=H[:, k * 128:(k + 1) * 128],
                                                                 identity=ident[:]),
                                 reads=[H, ident], writes=[P], sig=(k8 == 7))
                        dst = st[:, q * 8:(q + 1) * 8, ci * 128:(ci + 1) * 128]
                        if cp % 2 == 0:
                            S.op(V, lambda: nc.vector.tensor_copy(out=dst, in_=P[:]), reads=[P], writes=[st])
                        else:
                            S.op(A, lambda: nc.scalar.copy(out=dst, in_=P[:]), reads=[P], writes=[st])
                        cp += 1
                t0 = grp[0] * 128 - tok_off
                n = len(grp) * 128
                S.dma("sync", hT_ap.rearrange("k p t -> p k t")[:, :, t0:t0 + n], st[:, :, 0:n], reads=[st], writes=[wr])

    def load_aT(dst, src_ap, KT, t0, n, rd, step=8):
        for k0 in range(0, KT, step):
            k1 = min(KT, k0 + step)
            S.dma("sync", dst[:, k0:k1, 0:n], src_ap.rearrange("k p t -> p k t")[:, k0:k1, t0:t0 + n],
                  reads=[rd], writes=[dst])

    def wload(slot, src2d, nelem):
        for e0 in range(0, nelem, 8192):
            e1 = min(nelem, e0 + 8192)
            S.dma(G, slot[:, e0:e1], src2d[:, e0:e1], reads=[DB["in"]], writes=[slot])

    def mm_group(P_ap, Pt, KT, lhs_fn, rhs_fn, reads):
        for kt in range(KT):
            S.op(PE, lambda: nc.tensor.matmul(P_ap, lhsT=lhs_fn(kt), rhs=rhs_fn(kt), start=(kt == 0),
                                              stop=(kt == KT - 1)),
                 reads=reads, writes=[Pt], sig=(kt == KT - 1))

    def stage_inproj():
        with Ctx(nc, S) as C:
            cosT = C.sb("cosT", [128, NTE], F32)
            sinT = C.sb("sinT", [128, NTE], F32)
            prot = C.sb("prot", [128, 128], BF16)
            S.dma(G, prot[:], cst_d[5], writes=[prot])
            invf = C.sb("invf", [128, 1], F32)
            S.dma("sync", invf[:], invf_d, writes=[invf])
            with Ctx(nc, S) as C2:
                TWO_PI = 2.0 * math.pi
                C1 = 6.28125
                C2c = TWO_PI - C1
                MAGIC = 12582912.0
                for c0 in range(0, NTE, 1024):
                    ang = C2.sb("ang", [128, 1024], F32)
                    kk = C2.sb("kk", [128, 1024], F32)
                    rr = C2.sb("rr", [128, 1024], F32)
                    S.dma("sync", ang[:], pos_d[:, c0:c0 + 1024], writes=[ang])
                    S.op(V, lambda: nc.vector.tensor_scalar(out=ang[:], in0=ang[:], scalar1=invf[:], scalar2=None,
                                                            op0=ALU.mult), reads=[ang, invf], writes=[ang])
                    S.op(V, lambda: nc.vector.tensor_scalar(out=kk[:], in0=ang[:], scalar1=1.0 / TWO_PI, scalar2=MAGIC,
                                                            op0=ALU.mult, op1=ALU.add), reads=[ang], writes=[kk])
                    S.op(V, lambda: nc.vector.tensor_scalar(out=kk[:], in0=kk[:], scalar1=-MAGIC, scalar2=None,
                                                            op0=ALU.add), reads=[kk], writes=[kk])
                    S.op(V, lambda: nc.vector.scalar_tensor_tensor(out=rr[:], in0=kk[:], scalar=-C1, in1=ang[:],
                                                                   op0=ALU.mult, op1=ALU.add),
                         reads=[kk, ang], writes=[rr])
                    S.op(V, lambda: nc.vector.scalar_tensor_tensor(out=rr[:], in0=kk[:], scalar=-C2c, in1=rr[:],
                                                                   op0=ALU.mult, op1=ALU.add),
                         reads=[kk, rr], writes=[rr])
                    S.op(V, lambda: nc.vector.tensor_scalar(out=rr[:], in0=rr[:], scalar1=math.pi, scalar2=-math.pi,
                                                            op0=ALU.min, op1=ALU.max), reads=[rr], writes=[rr])
                    S.op(A, lambda: nc.scalar.activation(out=sinT[:, c0:c0 + 1024], in_=rr[:], func=AF.Sin),
                         reads=[rr], writes=[sinT])
                    S.op(A, lambda: nc.scalar.activation(out=kk[:], in_=rr[:], func=AF.Sin, scale=0.5),
                         reads=[rr], writes=[kk])
                    S.op(V, lambda: nc.vector.tensor_tensor(out=kk[:], in0=kk[:], in1=kk[:], op=ALU.mult),
                         reads=[kk], writes=[kk])
                    S.op(V, lambda: nc.vector.tensor_scalar(out=cosT[:, c0:c0 + 1024], in0=kk[:], scalar1=-2.0,
                                                            scalar2=1.0, op0=ALU.mult, op1=ALU.add),
                         reads=[kk], writes=[cosT])

            if s1mode == 0:
                return
            hT = C.sb("hT", [128, 32, 1152], BF16)
            CB = 256
            wsl = Rot([C.sb("wsl", [128, 32 * CB], BF16) for _ in range(2)])
            PS = Rot([C.ps("PS", [128, 512], F32) for _ in range(5)])
            PR = Rot([C.ps("PR", [128, 512], F32) for _ in range(2)])
            stg = Rot([C.sb("stg", [128, 512], BF16) for _ in range(3)])
            qraw = Rot([C.sb("qraw", [128, 512], BF16) for _ in range(2)])
            t1 = Rot([C.sb("t1", [128, 512], F32) for _ in range(2)])
            t2 = Rot([C.sb("t2", [128, 512], F32) for _ in range(2)])
            cnt = [0]

            fams = [("u", 0, 2048, 0)]
            for g in range(3):
                fams.append(("q%d" % g, 2048 + g * 1024, 1024, 15))
            kfirst = [14, 8, 0]
            for g in range(3):
                fams.append(("k%d" % g, 5120 + g * 1024, 1024, kfirst[g]))
            for g in range(3):
                fams.append(("v%d" % g, 8192 + g * 1024, 1024, kfirst[g]))
            fams.append(("gate", 11264, 8192, 15))

            sbs = [(0, 8), (8, 15), (15, 24), (24, 32)]
            if s1mode == 1:
                sbs, fams = sbs[:1], fams[:1]
            elif s1mode == 2:
                sbs = sbs[2:3]
                fams = [f_ for f_ in fams if f_[0][0] in "v"]
            elif s1mode == 3:
                sbs = sbs[2:3]
                fams = [f_ for f_ in fams if f_[0][0] in "q"]
            elif s1mode == 4:
                sbs = sbs[2:3]
                fams = [f_ for f_ in fams if f_[0][0] in "g"]
            elif s1mode == 8:
                sbs = sbs[3:4]
                fams = [("gate", 2048, 1024, 15)]
            elif s1mode == 9:
                sbs = sbs[3:4]
                fams = [f_ for f_ in fams if f_[0][0] in "q"]
            elif s1mode == 10:
                sbs = sbs[3:4]
                fams = [f_ for f_ in fams if f_[0][0] in "q"]
            elif s1mode == 11:
                sbs = sbs[3:4]
                fams = [f_ for f_ in fams if f_[0][0] in "v"]
            elif s1mode == 7:
                sbs = sbs[3:4]
                fams = [f_ for f_ in fams if f_[0][0] in "q"]
            elif s1mode == 5:
                sbs = sbs[2:3]
                fams = fams[:1]
            elif s1mode == 6:
                sbs = sbs[3:4]
                fams = [f_ for f_ in fams if f_[0][0] in "g"]
            for (sc0, sc1) in sbs:
                nt = (sc1 - sc0) * 128
                load_aT(hT, h1T_d, 32, sc0 * 128, nt, DB["h1T"])
                for (fname, fc0, fnc, first) in fams:
                    lo = max(sc0, first)
                    if lo >= sc1:
                        continue
                    tl0 = (lo - sc0) * 128
                    tn = (sc1 - lo) * 128
                    for cb0 in range(0, fnc, CB):
                        W = wsl.next()
                        wload(W, w_in[(fc0 + cb0) // CB], 32 * CB)
                        if fname[0] == "v":
                            g = int(fname[1])
                            for tc in range(tn // 128):
                                tl = tl0 + tc * 128
                                P = PS.next()
                                mm_group(P[:, 0:CB], P, 32, lambda kt: hT[:, kt, tl:tl + 128],
                                         lambda kt: W[:, kt * CB:(kt + 1) * CB], [hT, W])
                                sg = stg.next()
                                if cnt[0] % 2 == 0:
                                    S.op(V, lambda: nc.vector.tensor_copy(out=sg[:, 0:CB], in_=P[:, 0:CB]),
                                         reads=[P], writes=[sg])
                                else:
                                    S.op(A, lambda: nc.scalar.copy(out=sg[:, 0:CB], in_=P[:, 0:CB]),
                                         reads=[P], writes=[sg])
                                cnt[0] += 1
                                gi = sc0 * 128 + tl
                                if s1mode == 11:
                                    S.dma("sync", gT_d[g, :, 0:CB], sg[:, 0:CB], reads=[sg], writes=[DB["v"]])
                                else:
                                    S.dma("sync", v3_d[gi // 128, :, g * 1024 + cb0:g * 1024 + cb0 + CB], sg[:, 0:CB],
                                          reads=[sg], writes=[DB["v"]])
                            continue
                        for cc in range(CB // 128):
                            col = fc0 + cb0 + cc * 128
                            for (tg0, n) in tok_groups(tl0, tn):
                                P = PS.next()
                                mm_group(P[:, 0:n], P, 32, lambda kt: W[:, kt * CB + cc * 128:kt * CB + (cc + 1) * 128],
                                         lambda kt: hT[:, kt, tg0:tg0 + n], [hT, W])
                                gi = sc0 * 128 + tg0
                                sg = stg.next()
                                if fname == "u":
                                    if cnt[0] % 2 == 0:
                                        S.op(V, lambda: nc.vector.tensor_copy(out=sg[:, 0:n], in_=P[:, 0:n]),
                                             reads=[P], writes=[sg])
                                    else:
                                        S.op(A, lambda: nc.scalar.copy(out=sg[:, 0:n], in_=P[:, 0:n]),
                                             reads=[P], writes=[sg])
                                    cnt[0] += 1
                                    S.dma("sync", uT_d[col // 128, :, gi:gi + n], sg[:, 0:n], reads=[sg],
                                          writes=[DB["uT"]])
                                elif fname == "gate":
                                    S.op(A, lambda: nc.scalar.activation(out=sg[:, 0:n], in_=P[:, 0:n], func=AF.Sigmoid),
                                         reads=[P], writes=[sg])
                                    S.dma("sync", gT_d[((col - 11264) // 128) % 64, :, gi - EXT0:gi - EXT0 + n], sg[:, 0:n],
                                          reads=[sg], writes=[DB["gT"]])
                                else:
                                    qr = qraw.next()
                                    S.op(A, lambda: nc.scalar.copy(out=qr[:, 0:n], in_=P[:, 0:n]), reads=[P], writes=[qr])
                                    P2 = PR.next()
                                    if s1mode != 10:
                                        S.op(PE, lambda: nc.tensor.matmul(P2[:, 0:n], lhsT=prot[:], rhs=qr[:, 0:n],
                                                                          start=True, stop=True),
                                             reads=[prot, qr], writes=[P2])
                                    else:
                                        P2 = P
                                    a1 = t1.next()
                                    a2 = t2.next()
                                    S.op(V, lambda: nc.vector.tensor_tensor(out=a1[:, 0:n], in0=P[:, 0:n],
                                                                            in1=cosT[:, gi:gi + n], op=ALU.mult),
                                         reads=[P, cosT, qr], writes=[a1])
                                    S.op(V, lambda: nc.vector.tensor_tensor(out=a2[:, 0:n], in0=P2[:, 0:n],
                                                                            in1=sinT[:, gi:gi + n], op=ALU.mult),
                                         reads=[P2, sinT], writes=[a2])
                                    S.op(V, lambda: nc.vector.tensor_tensor(out=sg[:, 0:n], in0=a1[:, 0:n],
                                                                            in1=a2[:, 0:n], op=ALU.add),
                                         reads=[a1, a2], writes=[sg])
                                    if fname[0] == "q":
                                        hd = (col - 2048) // 128
                                        qdst = gT_d if s1mode == 9 else qT_d
                                        S.dma("sync", qdst[hd, :, gi - EXT0:gi - EXT0 + n], sg[:, 0:n], reads=[sg],
                                              writes=[DB["qT"]])
                                    else:
                                        hd = (col - 5120) // 128
                                        S.dma("sync", kT_d[hd, :, gi:gi + n], sg[:, 0:n], reads=[sg],
                                              writes=[DB["kT"]])

    def stage_ssm():
        with Ctx(nc, S) as C:
            tri = C.sb("tri", [128, 128], BF16)
            S.dma(G, tri[:], cst_d[0], writes=[tri])
            dsk = C.sb("dsk", [128, 16], F32)
            S.dma("sync", dsk[:], dsk_d, writes=[dsk])
            sidx = C.sb("sidx", [128, 1], F32)
            S.dma("sync", sidx[:], sidx_d, writes=[sidx])
            nsidx = C.sb("nsidx", [128, 1], F32)
            S.op(V, lambda: nc.vector.tensor_scalar(out=nsidx[:], in0=sidx[:], scalar1=-1.0, scalar2=None, op0=ALU.mult),
                 reads=[sidx], writes=[nsidx])
            pre_re = C.sb("pre_re", [128, 8192], BF16)
            pre_im = C.sb("pre_im", [128, 8192], BF16)
            post_re = C.sb("post_re", [128, 8192], BF16)
            post_im = C.sb("post_im", [128, 8192], BF16)
            l128_re = C.sb("l128_re", [128, 64], F32)
            l128_im = C.sb("l128_im", [128, 64], F32)
            TWO_PI = 2.0 * math.pi
            C1 = 6.28125
            C2c = TWO_PI - C1
            MAGIC = 12582912.0

            def sincos(C2, ang, n, sin_out, cos_out):
                kk = C2.sb("kk", [128, n], F32)
                S.op(V, lambda: nc.vector.tensor_scalar(out=kk[:], in0=ang[:], scalar1=1.0 / TWO_PI, scalar2=MAGIC,
                                                        op0=ALU.mult, op1=ALU.add), reads=[ang], writes=[kk])
                S.op(V, lambda: nc.vector.tensor_scalar(out=kk[:], in0=kk[:], scalar1=-MAGIC, scalar2=None,
                                                        op0=ALU.add), reads=[kk], writes=[kk])
                S.op(V, lambda: nc.vector.scalar_tensor_tensor(out=ang[:], in0=kk[:], scalar=-C1, in1=ang[:],
                                                               op0=ALU.mult, op1=ALU.add), reads=[kk, ang], writes=[ang])
                S.op(V, lambda: nc.vector.scalar_tensor_tensor(out=ang[:], in0=kk[:], scalar=-C2c, in1=ang[:],
                                                               op0=ALU.mult, op1=ALU.add), reads=[kk, ang], writes=[ang])
                S.op(V, lambda: nc.vector.tensor_scalar(out=ang[:], in0=ang[:], scalar1=math.pi, scalar2=-math.pi,
                                                        op0=ALU.min, op1=ALU.max), reads=[ang], writes=[ang])
                S.op(A, lambda: nc.scalar.activation(out=sin_out[:], in_=ang[:], func=AF.Sin), reads=[ang],
                     writes=[sin_out])
                S.op(A, lambda: nc.scalar.activation(out=kk[:], in_=ang[:], func=AF.Sin, scale=0.5), reads=[ang],
                     writes=[kk])
                S.op(V, lambda: nc.vector.tensor_tensor(out=kk[:], in0=kk[:], in1=kk[:], op=ALU.mult), reads=[kk],
                     writes=[kk])
                S.op(V, lambda: nc.vector.tensor_scalar(out=cos_out[:], in0=kk[:], scalar1=-2.0, scalar2=1.0,
                                                        op0=ALU.mult, op1=ALU.add), reads=[kk], writes=[cos_out])

            def tt(en, out, a, b, op, reads, writes):
                e = nc.vector if en == V else nc.gpsimd
                S.op(en, lambda: e.tensor_tensor(out=out, in0=a, in1=b, op=op), reads=reads, writes=writes)

            def cmul(C2, n, ar, ai, br, bi, outr, outi):
                m1 = C2.sb("m1", [128, n], F32)
                m2 = C2.sb("m2", [128, n], F32)
                tt(V, m1[:], ar[:], br[:], ALU.mult, [ar, br], [m1])
                tt(V, m2[:], ai[:], bi[:], ALU.mult, [ai, bi], [m2])
                tt(V, outr[:], m1[:], m2[:], ALU.subtract, [m1, m2], [outr])
                tt(V, m1[:], ar[:], bi[:], ALU.mult, [ar, bi], [m1])
                tt(V, m2[:], ai[:], br[:], ALU.mult, [ai, br], [m2])
                tt(V, outi[:], m1[:], m2[:], ALU.add, [m1, m2], [outi])

            def lam_f(C2, src_ap_fn, n):
                ldt = C2.sb("ldt", [128, n], F32)
                are = C2.sb("are", [128, n], F32)
                aim = C2.sb("aim", [128, n], F32)
                S.dma("sync", ldt[:], src_ap_fn(0), writes=[ldt])
                S.dma("sync", are[:], src_ap_fn(1), writes=[are])
                S.dma("sync", aim[:], src_ap_fn(2), writes=[aim])
                dt = C2.sb("dt", [128, n], F32)
                S.op(A, lambda: nc.scalar.activation(out=dt[:], in_=ldt[:], func=AF.Exp), reads=[ldt], writes=[dt])
                lnm = C2.sb("lnm", [128, n], F32)
                th = C2.sb("th", [128, n], F32)
                tt(V, lnm[:], dt[:], are[:], ALU.mult, [dt, are], [lnm])
                tt(V, th[:], dt[:], aim[:], ALU.mult, [dt, aim], [th])
                mag = C2.sb("mag", [128, n], F32)
                S.op(A, lambda: nc.scalar.activation(out=mag[:], in_=lnm[:], func=AF.Exp), reads=[lnm], writes=[mag])
                ang = C2.sb("ang", [128, n], F32)
                S.op(V, lambda: nc.vector.tensor_copy(out=ang[:], in_=th[:]), reads=[th], writes=[ang])
                sn = C2.sb("sn", [128, n], F32)
                cs = C2.sb("cs", [128, n], F32)
                sincos(C2, ang, n, sn, cs)
                lbr = C2.sb("lbr", [128, n], F32)
                lbi = C2.sb("lbi", [128, n], F32)
                tt(V, lbr[:], mag[:], cs[:], ALU.mult, [mag, cs], [lbr])
                tt(V, lbi[:], mag[:], sn[:], ALU.mult, [mag, sn], [lbi])
                S.op(V, lambda: nc.vector.tensor_scalar(out=lbr[:], in0=lbr[:], scalar1=-1.0, scalar2=None,
                                                        op0=ALU.add), reads=[lbr], writes=[lbr])
                den = C2.sb("den", [128, n], F32)
                d2 = C2.sb("d2", [128, n], F32)
                tt(V, den[:], are[:], are[:], ALU.mult, [are], [den])
                tt(V, d2[:], aim[:], aim[:], ALU.mult, [aim], [d2])
                tt(V, den[:], den[:], d2[:], ALU.add, [den, d2], [den])
                S.op(V, lambda: nc.vector.reciprocal(out=den[:], in_=den[:]), reads=[den], writes=[den])
                fr = C2.sb("fr", [128, n], F32)
                fi = C2.sb("fi", [128, n], F32)
                tt(V, fr[:], lbr[:], are[:], ALU.mult, [lbr, are], [fr])
                tt(V, d2[:], lbi[:], aim[:], ALU.mult, [lbi, aim], [d2])
                tt(V, fr[:], fr[:], d2[:], ALU.add, [fr, d2], [fr])
                tt(V, fr[:], fr[:], den[:], ALU.mult, [fr, den], [fr])
                tt(V, fi[:], lbi[:], are[:], ALU.mult, [lbi, are], [fi])
                tt(V, d2[:], lbr[:], aim[:], ALU.mult, [lbr, aim], [d2])
                tt(V, fi[:], fi[:], d2[:], ALU.subtract, [fi, d2], [fi])
                tt(V, fi[:], fi[:], den[:], ALU.mult, [fi, den], [fi])
                return lnm, th, fr, fi

            NS = 512
            for c0 in range(0, 8192, NS):
                with Ctx(nc, S) as C2:
                    lnm, th, fr, fi = lam_f(C2, lambda i: ssm_tm_d[i, :, c0:c0 + NS], NS)
                    mg = C2.sb("mg", [128, NS], F32)
                    S.op(A, lambda: nc.scalar.activation(out=mg[:], in_=lnm[:], func=AF.Exp, scale=nsidx[:]),
                         reads=[lnm, nsidx], writes=[mg])
                    ang = C2.sb("ang2", [128, NS], F32)
                    S.op(V, lambda: nc.vector.tensor_scalar(out=ang[:], in0=th[:], scalar1=nsidx[:], scalar2=None,
                                                            op0=ALU.mult), reads=[th, nsidx], writes=[ang])
                    sn = C2.sb("sn2", [128, NS], F32)
                    cs = C2.sb("cs2", [128, NS], F32)
                    sincos(C2, ang, NS, sn, cs)
                    er = C2.sb("er", [128, NS], F32)
                    ei = C2.sb("ei", [128, NS], F32)
                    tt(V, er[:], mg[:], cs[:], ALU.mult, [mg, cs], [er])
                    tt(V, ei[:], mg[:], sn[:], ALU.mult, [mg, sn], [ei])
                    outr = C2.sb("outr", [128, NS], F32)
                    outi = C2.sb("outi", [128, NS], F32)
                    cmul(C2, NS, er, ei, fr, fi, outr, outi)
                    S.op(V, lambda: nc.vector.tensor_copy(out=pre_re[:, c0:c0 + NS], in_=outr[:]), reads=[outr],
                         writes=[pre_re])
                    S.op(V, lambda: nc.vector.tensor_copy(out=pre_im[:, c0:c0 + NS], in_=outi[:]), reads=[outi],
                         writes=[pre_im])
            if ssm_mode == 0:
                return
            with Ctx(nc, S) as C2:
                lnm, th, fr, fi = lam_f(C2, lambda i: ssm_sm_d[i], 64)
                tidx = C2.sb("tidx", [128, 128], F32)
                S.dma("sync", tidx[:], tidx_d, writes=[tidx])
                mg = C2.sb("mg3", [128, 128], F32)
                ang = C2.sb("ang3", [128, 128], F32)
                sn = C2.sb("sn3", [128, 128], F32)
                cs = C2.sb("cs3", [128, 128], F32)
                for j in range(64):
                    S.op(A, lambda: nc.scalar.activation(out=mg[:], in_=tidx[:], func=AF.Exp, scale=lnm[:, j:j + 1]),
                         reads=[tidx, lnm], writes=[mg])
                    S.op(V, lambda: nc.vector.tensor_scalar(out=ang[:], in0=tidx[:], scalar1=th[:, j:j + 1],
                                                            scalar2=None, op0=ALU.mult), reads=[tidx, th], writes=[ang])
                    sincos_reuse(C2, ang, sn, cs)
                    tt(V, post_re[:, j * 128:(j + 1) * 128], mg[:], cs[:], ALU.mult, [mg, cs], [post_re])
                    tt(V, post_im[:, j * 128:(j + 1) * 128], mg[:], sn[:], ALU.mult, [mg, sn], [post_im])
                mg2 = C2.sb("mg4", [128, 64], F32)
                S.op(A, lambda: nc.scalar.activation(out=mg2[:], in_=lnm[:], func=AF.Exp, scale=128.0), reads=[lnm],
                     writes=[mg2])
                ang2 = C2.sb("ang4", [128, 64], F32)
                S.op(V, lambda: nc.vector.tensor_scalar(out=ang2[:], in0=th[:], scalar1=128.0, scalar2=None,
                                                        op0=ALU.mult), reads=[th], writes=[ang2])
                sn2 = C2.sb("sn4", [128, 64], F32)
                cs2 = C2.sb("cs4", [128, 64], F32)
                sincos(C2, ang2, 64, sn2, cs2)
                tt(V, l128_re[:], mg2[:], cs2[:], ALU.mult, [mg2, cs2], [l128_re])
                tt(V, l128_im[:], mg2[:], sn2[:], ALU.mult, [mg2, sn2], [l128_im])

            if ssm_mode == 1:
                return
            bblk = C.sb("bblk", [128, 16, 1024], BF16)
            for o8 in range(0, 16, 8):
                S.dma(G, bblk[:, o8:o8 + 8, :], bblk_d[:, o8:o8 + 8, :], writes=[bblk])
            cre = C.sb("cre", [128, 64, 128], BF16)
            cimn = C.sb("cimn", [128, 64, 128], BF16)
            S.dma(G, cre[:], cblk_d[0], writes=[cre])
            S.dma(G, cimn[:], cblk_d[1], writes=[cimn])
            S.op(V, lambda: nc.vector.tensor_scalar(out=cimn[:], in0=cimn[:], scalar1=-1.0, scalar2=None, op0=ALU.mult),
                 reads=[cimn], writes=[cimn])
            car_re = C.sb("car_re", [128, 64], F32)
            car_im = C.sb("car_im", [128, 64], F32)
            tot_re = C.sb("tot_re", [128, 64], F32)
            tot_im = C.sb("tot_im", [128, 64], F32)
            S.op(V, lambda: nc.vector.memset(car_re[:], 0.0), writes=[car_re])
            S.op(V, lambda: nc.vector.memset(car_im[:], 0.0), writes=[car_im])
            uTs = Rot([C.sb("uTs", [128, 16, 128], BF16) for _ in range(2)])
            m = [Rot([C.sb("m%d" % i, [128, 512], BF16) for _ in range(2)]) for i in range(4)]
            pm = [Rot([C.sb("pm%d" % i, [128, 512], BF16) for _ in range(2)]) for i in range(4)]
            crr = Rot([C.sb("crr", [128, 512], BF16) for _ in range(2)])
            cir = Rot([C.sb("cir", [128, 512], BF16) for _ in range(2)])
            trin = C.sb("trin", [128, 128], BF16)
            S.op(V, lambda: nc.vector.tensor_scalar(out=trin[:], in0=tri[:], scalar1=-1.0, scalar2=None, op0=ALU.mult),
                 reads=[tri], writes=[trin])
            vt_re = Rot([C.sb("vt_re", [128, 512], BF16) for _ in range(2)])
            vt_im = Rot([C.sb("vt_im", [128, 512], BF16) for _ in range(2)])
            xr = Rot([C.sb("xr", [128, 512], BF16) for _ in range(2)])
            xi = Rot([C.sb("xi", [128, 512], BF16) for _ in range(2)])
            ystg = Rot([C.sb("ystg", [128, 2048], BF16) for _ in range(2)])
            Y0 = C.sb("Y0", [128, 2048], F32)
            Y1 = C.sb("Y1", [128, 2048], F32)
            PBU = Rot([C.ps("PBU", [128, 1024], F32) for _ in range(1)])
            PCR = Rot([C.ps("PCR", [128, 4, 128], F32) for _ in range(2)])
            PCI = Rot([C.ps("PCI", [128, 4, 128], F32) for _ in range(2)])
            PY = Rot([C.ps("PY", [128, 512], F32) for _ in range(2)])
            U = None
            if ssm_mode == 2:
                return
            for c in ([0] if ssm_mode == 3 else ([15] if ssm_mode == 4 else range(32))):
                U = uTs.next()
                S.dma("sync", U[:], uT_d.rearrange("k p t -> p k t")[:, :, c * 128:c * 128 + 128],
                      reads=[DB["uT"]], writes=[U])
                cl = 0
                full = c >= 15
                ys = ystg.next() if full else None
                st_ = {}
                tk_ = {}

                def stA(o):
                    Pb = PBU.next()
                    for hf in range(2):
                        S.op(PE, lambda: nc.tensor.matmul(Pb[:, hf * 512:(hf + 1) * 512], lhsT=U[:, o, cl:cl + 128],
                                                          rhs=bblk[:, o, hf * 512:(hf + 1) * 512], start=True, stop=True),
                             reads=[U, bblk], writes=[Pb], sig=(hf == 1))
                    bre = Pb[:, 0:512]
                    bim = Pb[:, 512:1024]
                    er = pre_re[:, o * 512:(o + 1) * 512]
                    ei = pre_im[:, o * 512:(o + 1) * 512]
                    m0, m1, m2, m3 = (mm_.next() for mm_ in m)
                    tt(V, m0[:], bre, er, ALU.mult, [Pb, pre_re], [m0])
                    tt(V, m1[:], bim, ei, ALU.mult, [Pb, pre_im], [m1])
                    tt(V, m2[:], bim, er, ALU.mult, [Pb, pre_re], [m2])
                    tt(V, m3[:], bre, ei, ALU.mult, [Pb, pre_im], [m3])
                    vr = vt_re.next()
                    vi = vt_im.next()
                    tt(V, vr[:], m0[:], m1[:], ALU.subtract, [m0, m1], [vr])
                    tt(V, vi[:], m2[:], m3[:], ALU.add, [m2, m3], [vi])
                    Pr = PCR.next()
                    Pi = PCI.next()
                    c0_ = 0 if full else 126
                    ncol = 128 if full else 2
                    for j4 in range(4):
                        blk = slice(j4 * 128, (j4 + 1) * 128)
                        S.op(PE, lambda: nc.tensor.matmul(Pr[:, j4, 0:ncol], lhsT=vr[:, blk], rhs=tri[:, c0_:128],
                                                          start=True, stop=True), reads=[vr, tri], writes=[Pr],
                             sig=(j4 == 3))
                    for j4 in range(4):
                        blk = slice(j4 * 128, (j4 + 1) * 128)
                        S.op(PE, lambda: nc.tensor.matmul(Pi[:, j4, 0:ncol], lhsT=vi[:, blk], rhs=tri[:, c0_:128],
                                                          start=True, stop=True), reads=[vi, tri], writes=[Pi],
                             sig=(j4 == 3))
                    st_[o] = (Pr, Pi)

                def stT(o):
                    Pr, Pi = st_[o]
                    lc = 127 if full else 1
                    tk_[o] = Buf("tk")
                    tt(V, tot_re[:, o * 4:(o + 1) * 4], Pr[:, :, lc], car_re[:, o * 4:(o + 1) * 4], ALU.add,
                       [Pr, car_re], [tot_re, tk_[o]])
                    tt(V, tot_im[:, o * 4:(o + 1) * 4], Pi[:, :, lc], car_im[:, o * 4:(o + 1) * 4], ALU.add,
                       [Pi, car_im], [tot_im, tk_[o]])

                def stB(o):
                    Pr, Pi = st_[o]
                    A0, A1, A2, A3 = (p_.next() for p_ in pm)
                    CR = crr.next()
                    CI = cir.next()
                    for j4 in range(4):
                        j = o * 4 + j4
                        blk = slice(j4 * 128, (j4 + 1) * 128)
                        S.op(A, lambda: nc.scalar.activation(out=CR[:, blk], in_=Pr[:, j4, :], func=AF.Identity,
                                                             bias=car_re[:, j:j + 1]), reads=[Pr, car_re, tk_[o]],
                             writes=[CR])
                        S.op(A, lambda: nc.scalar.activation(out=CI[:, blk], in_=Pi[:, j4, :], func=AF.Identity,
                                                             bias=car_im[:, j:j + 1]), reads=[Pi, car_im, tk_[o]],
                             writes=[CI])
                    osl = slice(o * 512, (o + 1) * 512)
                    tt(V, A0[:], CR[:], post_re[:, osl], ALU.mult, [CR, post_re], [A0])
                    tt(V, A1[:], CI[:], post_im[:, osl], ALU.mult, [CI, post_im], [A1])
                    tt(V, A2[:], CR[:], post_im[:, osl], ALU.mult, [CR, post_im], [A2])
                    tt(V, A3[:], CI[:], post_re[:, osl], ALU.mult, [CI, post_re], [A3])
                    X_r = xr.next()
                    X_i = xi.next()
                    tt(V, X_r[:], A0[:], A1[:], ALU.subtract, [A0, A1], [X_r])
                    tt(V, X_i[:], A2[:], A3[:], ALU.add, [A2, A3], [X_i])
                    Py = PY.next()
                    for j4 in range(4):
                        j = o * 4 + j4
                        blk = slice(j4 * 128, (j4 + 1) * 128)
                        S.op(PE, lambda: nc.tensor.matmul(Py[:, 0:128], lhsT=cre[:, j, :], rhs=X_r[:, blk],
                                                          start=(j4 == 0), stop=False), reads=[cre, X_r], writes=[Py],
                             sig=False)
                        S.op(PE, lambda: nc.tensor.matmul(Py[:, 0:128], lhsT=cimn[:, j, :], rhs=X_i[:, blk],
                                                          start=False, stop=(j4 == 3)), reads=[cimn, X_i], writes=[Py],
                             sig=(j4 == 3))
                    st_[o] = Py

                def stC(o):
                    Py = st_[o]
                    S.op(V, lambda: nc.vector.scalar_tensor_tensor(out=Y0[:, o * 128:(o + 1) * 128],
                                                                   in0=U[:, o, cl:cl + 128], scalar=dsk[:, o:o + 1],
                                                                   in1=Py[:, 0:128], op0=ALU.mult, op1=ALU.add),
                         reads=[U, dsk, Py], writes=[Y0])

                if not full:
                    for step in range(16 + 1):
                        if step < 16:
                            stA(step)
                        if step - 1 >= 0:
                            stT(step - 1)
                else:
                    for step in range(16 + 2):
                        if step < 16:
                            stA(step)
                        if 0 <= step - 1 < 16:
                            stB(step - 1)
                        if step < 16:
                            stT(step)
                        if 0 <= step - 2 < 16:
                            stC(step - 2)
                    S.op(A, lambda: nc.scalar.activation(out=Y1[:], in_=Y0[:], func=AF.Square), reads=[Y0], writes=[Y1])
                    S.op(V, lambda: nc.vector.tensor_scalar(out=Y1[:], in0=Y1[:], scalar1=0.044715, scalar2=1.0,
                                                            op0=ALU.mult, op1=ALU.add), reads=[Y1], writes=[Y1])
                    tt(V, Y1[:], Y1[:], Y0[:], ALU.mult, [Y1, Y0], [Y1])
                    S.op(A, lambda: nc.scalar.activation(out=Y1[:], in_=Y1[:], func=AF.Sigmoid, scale=1.5957691216),
                         reads=[Y1], writes=[Y1])
                    tt(V, ys[:], Y0[:], Y1[:], ALU.mult, [Y0, Y1], [ys])
                cmul_small(tot_re, tot_im, l128_re, l128_im, car_re, car_im)
                if full:
                    e0 = c * 128 - EXT0
                    S.dma("sync", yT_d.rearrange("k p t -> p k t")[:, :, e0:e0 + 128],
                          ys[:].rearrange("p (k t) -> p k t", k=16), reads=[ys], writes=[DB["yT"]])

    def cmul_small(ar, ai, br, bi, outr, outi):
        m1, m2 = _cm[0], _cm[1]
        S.op(V, lambda: nc.vector.tensor_tensor(out=m1[:], in0=ar[:], in1=br[:], op=ALU.mult), reads=[ar, br], writes=[m1])
        S.op(V, lambda: nc.vector.tensor_tensor(out=m2[:], in0=ai[:], in1=bi[:], op=ALU.mult), reads=[ai, bi], writes=[m2])
        S.op(V, lambda: nc.vector.tensor_tensor(out=outr[:], in0=m1[:], in1=m2[:], op=ALU.subtract), reads=[m1, m2],
             writes=[outr])
        S.op(V, lambda: nc.vector.tensor_tensor(out=m1[:], in0=ar[:], in1=bi[:], op=ALU.mult), reads=[ar, bi], writes=[m1])
        S.op(V, lambda: nc.vector.tensor_tensor(out=m2[:], in0=ai[:], in1=br[:], op=ALU.mult), reads=[ai, br], writes=[m2])
        S.op(V, lambda: nc.vector.tensor_tensor(out=outi[:], in0=m1[:], in1=m2[:], op=ALU.add), reads=[m1, m2],
             writes=[outi])

    def sincos_reuse(C2, ang, sn, cs):
        kk = _cm["kk"]
        TWO_PI = 2.0 * math.pi
        C1 = 6.28125
        C2c = TWO_PI - C1
        MAGIC = 12582912.0
        S.op(V, lambda: nc.vector.tensor_scalar(out=kk[:], in0=ang[:], scalar1=1.0 / TWO_PI, scalar2=MAGIC,
                                                op0=ALU.mult, op1=ALU.add), reads=[ang], writes=[kk])
        S.op(V, lambda: nc.vector.tensor_scalar(out=kk[:], in0=kk[:], scalar1=-MAGIC, scalar2=None, op0=ALU.add),
             reads=[kk], writes=[kk])
        S.op(V, lambda: nc.vector.scalar_tensor_tensor(out=ang[:], in0=kk[:], scalar=-C1, in1=ang[:], op0=ALU.mult,
                                                       op1=ALU.add), reads=[kk, ang], writes=[ang])
        S.op(V, lambda: nc.vector.scalar_tensor_tensor(out=ang[:], in0=kk[:], scalar=-C2c, in1=ang[:], op0=ALU.mult,
                                                       op1=ALU.add), reads=[kk, ang], writes=[ang])
        S.op(V, lambda: nc.vector.tensor_scalar(out=ang[:], in0=ang[:], scalar1=math.pi, scalar2=-math.pi,
                                                op0=ALU.min, op1=ALU.max), reads=[ang], writes=[ang])
        S.op(A, lambda: nc.scalar.activation(out=sn[:], in_=ang[:], func=AF.Sin), reads=[ang], writes=[sn])
        S.op(A, lambda: nc.scalar.activation(out=kk[:], in_=ang[:], func=AF.Sin, scale=0.5), reads=[ang], writes=[kk])
        S.op(V, lambda: nc.vector.tensor_tensor(out=kk[:], in0=kk[:], in1=kk[:], op=ALU.mult), reads=[kk], writes=[kk])
        S.op(V, lambda: nc.vector.tensor_scalar(out=cs[:], in0=kk[:], scalar1=-2.0, scalar2=1.0, op0=ALU.mult,
                                                op1=ALU.add), reads=[kk], writes=[cs])

    def stage_attn():
        with Ctx(nc, S) as C:
            maskp = C.sb("maskp", [128, 128], BF16)
            maskc = C.sb("maskc", [128, 128], BF16)
            validh = C.sb("validh", [128, 128], BF16)
            ones = C.sb("ones", [128, 128], BF16)
            S.dma(G, maskp[:], cst_d[2], writes=[maskp])
            S.dma(G, maskc[:], cst_d[3], writes=[maskc])
            S.dma(G, validh[:], cst_d[4], writes=[validh])
            S.dma(G, ones[:], ones_d, writes=[ones])
            qs = Rot([C.sb("qs", [128, NEXT], BF16) for _ in range(2)])
            ks = Rot([C.sb("ks", [128, NTE], BF16) for _ in range(2)])
            vs = Rot([C.sb("vs", [128, 18, 128], BF16) for _ in range(3)])
            num = C.sb("num", [128, NEXT], F32)
            den = C.sb("den", [128, NEXT], F32)
            ob = Rot([C.sb("ob", [128, NEXT], BF16) for _ in range(2)])
            es = Rot([C.sb("es", [128, 256], BF16) for _ in range(5)])
            PSs = Rot([C.ps("PSs", [128, 512], F32) for _ in range(4)])
            PO = Rot([C.ps("PO", [128, 512], F32) for _ in range(2)])
            PD = Rot([C.ps("PD", [128, 512], F32) for _ in range(2)])
            scale = 1.0 / math.sqrt(128.0)
            dil = [1, 4, 16]
            pend = []
            for h in range(8):
                for g in range(3):
                    d = dil[g]
                    hd = g * 8 + h
                    Q = qs.next()
                    K = ks.next()
                    S.dma("sync", Q[:], qT_d[hd], reads=[DB["qT"]], writes=[Q])
                    S.dma("sync", K[:], kT_d[hd], reads=[DB["kT"]], writes=[K])
                    nclass = NTE // d
                    nblk = nclass // 128
                    m_lo = EXT0 // d
                    nb_lo = m_lo // 128
                    kb_lo = max(0, nb_lo - 1)
                    nkb = nblk - kb_lo
                    for r in range(d):
                        Vt = vs.next()
                        vsrc = v_d.rearrange("(kb j dd) c -> j kb dd c", j=128, dd=d)[:, kb_lo:nblk, r,
                                                                                        hd * 128:(hd + 1) * 128]
                        S.dma("sync", Vt[:, 0:nkb, :], vsrc, reads=[DB["v"]], writes=[Vt])
                        for nb in range(nb_lo, nblk):
                            qi_min = max(0, -(-(EXT0 - r) // d) - 128 * nb)
                            if qi_min >= 128:
                                continue
                            q0 = (128 * nb) * d + r - EXT0
                            qa = q0 + qi_min * d
                            nq = 128 - qi_min
                            q_ap = Q[:, qa:qa + (nq - 1) * d + 1:d]
                            Ps = PSs.next()
                            blocks = []
                            if nb - 1 >= kb_lo and nb - 1 >= 0:
                                blocks.append((nb - 1, maskp, 0))
                            blocks.append((nb, maskc, 1))
                            E = es.next()
                            for (kb, msk, slot) in blocks:
                                k0 = (128 * kb) * d + r
                                k_ap = K[:, k0:k0 + 127 * d + 1:d]
                                S.op(PE, lambda: nc.tensor.matmul(Ps[:, slot * 128:slot * 128 + nq], lhsT=k_ap, rhs=q_ap,
                                                                  start=True, stop=False), reads=[K, Q], writes=[Ps],
                                     sig=False)
                                S.op(PE, lambda: nc.tensor.matmul(Ps[:, slot * 128:slot * 128 + nq], lhsT=ident[:],
                                                                  rhs=msk[:, qi_min:128], start=False, stop=True),
                                     reads=[ident, msk], writes=[Ps], sig=True)
                                S.op(A, lambda: nc.scalar.activation(out=E[:, slot * 128:slot * 128 + nq],
                                                                     in_=Ps[:, slot * 128:slot * 128 + nq], func=AF.Exp,
                                                                     scale=scale), reads=[Ps], writes=[E])
                            def P_fn(blocks=blocks, E=E, Vt=Vt, nq=nq, qa=qa, d=d, g=g, kb_lo=kb_lo, r=r):
                                Po = PO.next()
                                Pd = PD.next()
                                for bi, (kb, msk, slot) in enumerate(blocks):
                                    S.op(PE, lambda: nc.tensor.matmul(Po[:, 0:nq], lhsT=Vt[:, kb - kb_lo, :],
                                                                      rhs=E[:, slot * 128:slot * 128 + nq], start=(bi == 0),
                                                                      stop=(bi == len(blocks) - 1)), reads=[Vt, E],
                                         writes=[Po], sig=(bi == len(blocks) - 1))
                                for bi, (kb, msk, slot) in enumerate(blocks):
                                    key_halo = ((128 * kb) * d + r) < OWN0
                                    vm = validh if key_halo else ones
                                    S.op(PE, lambda: nc.tensor.matmul(Pd[:, 0:nq], lhsT=vm[:],
                                                                      rhs=E[:, slot * 128:slot * 128 + nq], start=(bi == 0),
                                                                      stop=(bi == len(blocks) - 1)), reads=[vm, E],
                                         writes=[Pd], sig=(bi == len(blocks) - 1))
                                n_ap = num[:, qa:qa + (nq - 1) * d + 1:d]
                                d_ap = den[:, qa:qa + (nq - 1) * d + 1:d]
                                if g == 0:
                                    S.op(V, lambda: nc.vector.tensor_copy(out=n_ap, in_=Po[:, 0:nq]), reads=[Po], writes=[num])
                                    S.op(V, lambda: nc.vector.tensor_copy(out=d_ap, in_=Pd[:, 0:nq]), reads=[Pd], writes=[den])
                                else:
                                    S.op(V, lambda: nc.vector.tensor_tensor(out=n_ap, in0=Po[:, 0:nq], in1=n_ap, op=ALU.add),
                                         reads=[Po, num], writes=[num])
                                    S.op(V, lambda: nc.vector.tensor_tensor(out=d_ap, in0=Pd[:, 0:nq], in1=d_ap, op=ALU.add),
                                         reads=[Pd, den], writes=[den])

                            pend.append(P_fn)
                            if len(pend) > 2:
                                pend.pop(0)()
                while pend:
                    pend.pop(0)()
                S.op(V, lambda: nc.vector.tensor_scalar(out=den[:], in0=den[:], scalar1=1e-30, scalar2=None, op0=ALU.max),
                     reads=[den], writes=[den])
                S.op(V, lambda: nc.vector.reciprocal(out=den[:], in_=den[:]), reads=[den], writes=[den])
                O = ob.next()
                S.op(V, lambda: nc.vector.tensor_tensor(out=O[:], in0=num[:], in1=den[:], op=ALU.mult), reads=[num, den],
                     writes=[O])
                S.dma("sync", attnT_d[h], O[:], reads=[O], writes=[DB["attnT"]])

    def stage_mix():
        with Ctx(nc, S) as C:
            yTs = C.sb("yTs", [128, 16, 1152], BF16)
            aTs = C.sb("aTs", [128, 8, 1152], BF16)
            wg = Rot([C.sb("wg", [128, 16 * 2 * 512], BF16) for _ in range(2)])
            wa = Rot([C.sb("wa", [128, 8 * 512], BF16) for _ in range(2)])
            gs = Rot([C.sb("gs", [128, 1152], BF16) for _ in range(2)])
            ga = Rot([C.sb("ga", [128, 1152], BF16) for _ in range(2)])
            sgm = Rot([C.sb("sgm", [128, 512], F32) for _ in range(2)])
            ta = Rot([C.sb("ta", [128, 512], F32) for _ in range(2)])
            tb = Rot([C.sb("tb", [128, 512], F32) for _ in range(2)])
            mst = Rot([C.sb("mst", [128, 512], BF16) for _ in range(3)])
            PA = Rot([C.ps("PA", [128, 512], F32) for _ in range(2)])
            PB = Rot([C.ps("PB", [128, 512], F32) for _ in range(2)])
            PC = Rot([C.ps("PC", [128, 512], F32) for _ in range(2)])
            for (e0, nt) in ((0, 1152), (1152, 1024)):
                load_aT(yTs, yT_d, 16, e0, nt, DB["yT"])
                load_aT(aTs, attnT_d, 8, e0, nt, DB["attnT"])
                for jb in range(8):
                    Wg = wg.next()
                    Wa = wa.next()
                    wload(Wg, w_glu[jb], 16 * 2 * 512)
                    wload(Wa, w_ao[jb], 8 * 512)
                    for jj in range(4):
                        j = jb * 4 + jj
                        Gs = gs.next()
                        Ga = ga.next()
                        S.dma("sync", Gs[:, 0:nt], gT_d[j, :, e0:e0 + nt], reads=[DB["gT"]], writes=[Gs])
                        S.dma("sync", Ga[:, 0:nt], gT_d[32 + j, :, e0:e0 + nt], reads=[DB["gT"]], writes=[Ga])
                        for (t0, n) in tok_groups(0, nt):
                            Pa = PA.next()
                            Pb = PB.next()
                            Pc = PC.next()
                            mm_group(Pa[:, 0:n], Pa, 16, lambda kt: Wg[:, kt * 1024 + jj * 128:kt * 1024 + (jj + 1) * 128],
                                     lambda kt: yTs[:, kt, t0:t0 + n], [Wg, yTs])
                            mm_group(Pb[:, 0:n], Pb, 16, lambda kt: Wg[:, kt * 1024 + 512 + jj * 128:kt * 1024 + 512 + (jj + 1) * 128],
                                     lambda kt: yTs[:, kt, t0:t0 + n], [Wg, yTs])
                            mm_group(Pc[:, 0:n], Pc, 8, lambda kt: Wa[:, kt * 512 + jj * 128:kt * 512 + (jj + 1) * 128],
                                     lambda kt: aTs[:, kt, t0:t0 + n], [Wa, aTs])
                            sg_ = sgm.next()
                            a_ = ta.next()
                            b_ = tb.next()
                            o_ = mst.next()
                            S.op(A, lambda: nc.scalar.activation(out=sg_[:, 0:n], in_=Pb[:, 0:n], func=AF.Sigmoid),
                                 reads=[Pb], writes=[sg_])
                            S.op(V, lambda: nc.vector.tensor_tensor(out=a_[:, 0:n], in0=Pa[:, 0:n], in1=sg_[:, 0:n],
                                                                    op=ALU.mult), reads=[Pa, sg_], writes=[a_])
                            S.op(V, lambda: nc.vector.tensor_tensor(out=a_[:, 0:n], in0=a_[:, 0:n], in1=Gs[:, t0:t0 + n],
                                                                    op=ALU.mult), reads=[a_, Gs], writes=[a_])
                            S.op(V, lambda: nc.vector.tensor_tensor(out=b_[:, 0:n], in0=Pc[:, 0:n], in1=Ga[:, t0:t0 + n],
                                                                    op=ALU.mult), reads=[Pc, Ga], writes=[b_])
                            S.op(V, lambda: nc.vector.tensor_tensor(out=o_[:, 0:n], in0=a_[:, 0:n], in1=b_[:, 0:n],
                                                                    op=ALU.add), reads=[a_, b_], writes=[o_])
                            S.dma("sync", mixT_d[j, :, e0 + t0:e0 + t0 + n], o_[:, 0:n], reads=[o_],
                                  writes=[DB["mixT"]])

    def stage_wout():
        with Ctx(nc, S) as C:
            mT = C.sb("mT", [128, 32, 1152], BF16)
            CB = 512
            wsl = Rot([C.sb("wsl2", [128, 32 * CB], BF16) for _ in range(2)])
            xin = Rot([C.sb("xin", [128, CB], F32) for _ in range(3)])
            xo = Rot([C.sb("xo", [128, CB], F32) for _ in range(3)])
            PS = Rot([C.ps("PS2", [128, 512], F32) for _ in range(4)])
            for (e0, nt) in ((0, 1152), (1152, 1024)):
                load_aT(mT, mixT_d, 32, e0, nt, DB["mixT"])
                for cb in range(D // CB):
                    W = wsl.next()
                    wload(W, w_out[cb], 32 * CB)
                    for tc in range(nt // 128):
                        tl = tc * 128
                        Xi = xin.next()
                        gi = EXT0 + e0 + tl
                        S.dma("sync", Xi[:], x_d[gi:gi + 128, cb * CB:(cb + 1) * CB], reads=[DB["in"]], writes=[Xi])
                        P = PS.next()
                        mm_group(P[:], P, 32, lambda kt: mT[:, kt, tl:tl + 128], lambda kt: W[:, kt * CB:(kt + 1) * CB], [mT, W])
                        Xo = xo.next()
                        S.op(V, lambda: nc.vector.tensor_tensor(out=Xo[:], in0=P[:], in1=Xi[:], op=ALU.add),
                             reads=[P, Xi], writes=[Xo])
                        S.dma("sync", x1_3d[(e0 + tl) // 128, :, cb * CB:(cb + 1) * CB], Xo[:], reads=[Xo],
                              writes=[DB["x1"]])

    def stage_ffn_up():
        with Ctx(nc, S) as C:
            cvp = C.sb("cvp", [128, 4, 2 * NFB], F32)
            S.dma("sync", cvp[:], convp_d, writes=[cvp])
            HALO = 32
            NTK = HALO + 1024
            hT = C.sb("hT2", [128, 32, NTK], BF16)
            wsl = Rot([C.sb("wsl3", [128, 32 * 2 * 256], BF16) for _ in range(2)])
            raw = [Rot([C.sb("raw%d" % i, [128, NTK], F32) for _ in range(2)]) for i in range(2)]
            cv = [Rot([C.sb("cv%d" % i, [128, 1024], F32) for _ in range(2)]) for i in range(2)]
            ast = Rot([C.sb("ast", [128, 1024], BF16) for _ in range(2)])
            PS = Rot([C.ps("PS3", [128, 512], F32) for _ in range(6)])
            for sbi in range(2):
                e_first = 128 + sbi * 1024 - HALO
                load_aT(hT, h2T_d, 32, e_first, NTK, DB["h2T"])
                for jb in range(NFB // 2):
                    W = wsl.next()
                    wload(W, w_up[jb], 32 * 2 * 256)
                    for jj in range(2):
                        j = jb * 2 + jj
                        rws = []
                        for ag in range(2):
                            R = raw[ag].next()
                            rws.append(R)
                            for gi_, (t0, n) in enumerate([(0, HALO), (HALO, 512), (HALO + 512, 512)]):
                                P = PS.next()
                                mm_group(P[:, 0:n], P, 32, lambda kt: W[:, kt * 512 + ag * 256 + jj * 128:kt * 512 + ag * 256 + (jj + 1) * 128],
                                         lambda kt: hT[:, kt, t0:t0 + n], [W, hT])
                                if gi_ % 2 == 0:
                                    S.op(A, lambda: nc.scalar.copy(out=R[:, t0:t0 + n], in_=P[:, 0:n]), reads=[P],
                                         writes=[R])
                                else:
                                    S.op(V, lambda: nc.vector.tensor_copy(out=R[:, t0:t0 + n], in_=P[:, 0:n]), reads=[P],
                                         writes=[R])
                        cvs = []
                        for ag in range(2):
                            R = rws[ag]
                            Cv = cv[ag].next()
                            cvs.append(Cv)
                            col = ag * NFB + j
                            S.op(A, lambda: nc.scalar.activation(out=Cv[:], in_=R[:, HALO:NTK], func=AF.Identity,
                                                                 scale=cvp[:, 2, col:col + 1],
                                                                 bias=cvp[:, 3, col:col + 1]), reads=[R, cvp], writes=[Cv])
                            S.op(V, lambda: nc.vector.scalar_tensor_tensor(out=Cv[:], in0=R[:, HALO - 1:NTK - 1],
                                                                           scalar=cvp[:, 1, col:col + 1], in1=Cv[:],
                                                                           op0=ALU.mult, op1=ALU.add),
                                 reads=[R, cvp, Cv], writes=[Cv])
                            S.op(V, lambda: nc.vector.scalar_tensor_tensor(out=Cv[:], in0=R[:, HALO - 2:NTK - 2],
                                                                           scalar=cvp[:, 0, col:col + 1], in1=Cv[:],
                                                                           op0=ALU.mult, op1=ALU.add),
                                 reads=[R, cvp, Cv], writes=[Cv])
                        Ca, Cg = cvs
                        S.op(A, lambda: nc.scalar.activation(out=Ca[:], in_=Ca[:], func=AF.Silu), reads=[Ca], writes=[Ca])
                        O = ast.next()
                        S.op(V, lambda: nc.vector.tensor_tensor(out=O[:], in0=Ca[:], in1=Cg[:], op=ALU.mult),
                             reads=[Ca, Cg], writes=[O])
                        S.dma("sync", actT_d[j, :, sbi * 1024:(sbi + 1) * 1024], O[:], reads=[O], writes=[DB["actT"]])

    def stage_ffn_down():
        with Ctx(nc, S) as C:
            aT = C.sb("aT7", [128, NFB, 512], BF16)
            pieces = [(0, 22), (22, 22), (44, 21), (65, 21)]
            wsl = Rot([C.sb("wsl4", [128, 22 * 512], BF16) for _ in range(3)])
            xin = Rot([C.sb("xin7", [128, 512], F32) for _ in range(3)])
            xo = Rot([C.sb("xo7", [128, 512], F32) for _ in range(3)])
            junk = C.sb("junk7", [128, 512], BF16)
            PS = Rot([C.ps("PS4", [128, 512], F32) for _ in range(8)])
            for tb in range(4):
                load_aT(aT, actT_d, NFB, tb * 512, 512, DB["actT"], step=11)
                for cb in range(8):
                    Ps = [PS.next() for _ in range(4)]
                    for pi, (k0, kn) in enumerate(pieces):
                        W = wsl.next()
                        wload(W, w_down[cb, :, k0 * 512:(k0 + kn) * 512], kn * 512)
                        for tc in range(4):
                            P = Ps[tc]
                            for kk in range(kn):
                                kt = k0 + kk
                                S.op(PE, lambda: nc.tensor.matmul(P[:], lhsT=aT[:, kt, tc * 128:(tc + 1) * 128],
                                                                  rhs=W[:, kk * 512:(kk + 1) * 512], start=(kt == 0), stop=(kt == NFB - 1)),
                                     reads=[aT, W], writes=[P], sig=(kk == kn - 1))
                    for tc in range(4):
                        P = Ps[tc]
                        ch = tb * 4 + tc
                        Xi = xin.next()
                        S.dma("sync", Xi[:], x1_3d[1 + ch, :, cb * 512:(cb + 1) * 512],
                              reads=[DB["x1"]], writes=[Xi])
                        Xo = xo.next()
                        S.op(V, lambda: nc.vector.tensor_tensor(out=Xo[:], in0=P[:], in1=Xi[:], op=ALU.add),
                             reads=[P, Xi], writes=[Xo])
                        S.op(A, lambda: nc.scalar.activation(out=junk[:], in_=Xo[:], func=AF.Square,
                                                             accum_out=ssq[:, ch, cb:cb + 1]), reads=[Xo],
                             writes=[junk, ssq])
                        S.dma("sync", x2_3d[ch, :, cb * 512:(cb + 1) * 512], Xo[:], reads=[Xo],
                              writes=[DB["x2"]])
        with Ctx(nc, S) as C:
            gbc = C.sb("gbcf", [128, D], F32)
            S.dma("sync", gbc[:], g3_d[2], writes=[gbc])
            xt = Rot([C.sb("xt8", [128, D], F32) for _ in range(2)])
            yo = Rot([C.sb("yo8", [128, D], F32) for _ in range(2)])
            rs = Rot([C.sb("rs8", [128, 1], F32) for _ in range(2)])
            for ch in range(16):
                X = xt.next()
                S.dma("sync", X[:], x2_d[ch * 128:(ch + 1) * 128, :], reads=[DB["x2"]], writes=[X])
                r_ = rs.next()
                S.op(V, lambda: nc.vector.tensor_reduce(out=r_[:], in_=ssq[:, ch, :], axis=mybir.AxisListType.X,
                                                        op=ALU.add), reads=[ssq], writes=[r_])
                S.op(V, lambda: nc.vector.tensor_scalar(out=r_[:], in0=r_[:], scalar1=1.0 / D, scalar2=RMS_EPS,
                                                        op0=ALU.mult, op1=ALU.add), reads=[r_], writes=[r_])
                S.op(A, lambda: nc.scalar.activation(out=r_[:], in_=r_[:], func=AF.Sqrt), reads=[r_], writes=[r_])
                S.op(V, lambda: nc.vector.reciprocal(out=r_[:], in_=r_[:]), reads=[r_], writes=[r_])
                Y = yo.next()
                S.op(V, lambda: nc.vector.scalar_tensor_tensor(out=Y[:], in0=X[:], scalar=r_[:], in1=gbc[:],
                                                               op0=ALU.mult, op1=ALU.mult), reads=[X, r_, gbc],
                     writes=[Y])
                S.dma("sync", out_d[ch * 128:(ch + 1) * 128, :], Y[:], reads=[Y], writes=[DB["out"]])

    stages = [lambda: stage_normT(x_d, list(range(32)), 0, h1T_d, 0, DB["in"], DB["h1T"]),
              stage_inproj, stage_ssm, stage_attn, stage_mix, stage_wout,
              lambda: stage_normT(x1_d, list(range(17)), 1, h2T_d, 0, DB["x1"], DB["h2T"]),
              stage_ffn_up, stage_ffn_down]
    for si, st in enumerate(stages[:nstages]):
        if sel is None or si in sel:
            st()
    S.barrier()
    top.close()
    return nc


def _bf(x):
    return x


def host_layout(inp):
    f = np.float32
    x = np.asarray(inp["x"], f)
    sh = {}
    def tile_w(W, cbw):
        K, N = W.shape
        return np.ascontiguousarray(W.reshape(K // 128, 128, N // cbw, cbw).transpose(2, 1, 0, 3)).reshape(
            N // cbw, 128, (K // 128) * cbw)

    sh["w_in"] = tile_w(np.asarray(inp["w_in"], f)[0], 256)
    sh["w_ao"] = tile_w(np.asarray(inp["w_attn_out"], f)[0], 512)
    sh["w_out"] = tile_w(np.asarray(inp["w_out"], f)[0], 512)
    sh["w_down"] = tile_w(np.asarray(inp["w_down"], f)[0], 512)
    wg_ = np.asarray(inp["w_glu"], f)[0].reshape(16, 128, 2, 8, 512)
    sh["w_glu"] = np.ascontiguousarray(wg_.transpose(3, 1, 0, 2, 4)).reshape(8, 128, 16 * 2 * 512)
    wu_ = np.asarray(inp["w_up"], f)[0].reshape(32, 128, 2, NFB // 2, 256)
    sh["w_up"] = np.ascontiguousarray(wu_.transpose(3, 1, 0, 2, 4)).reshape(NFB // 2, 128, 32 * 2 * 256)
    g3 = np.stack([np.asarray(inp["g_mix"], f)[0], np.asarray(inp["g_ffn"], f)[0], np.asarray(inp["g_final"], f)])
    sh["g3"] = np.ascontiguousarray(np.broadcast_to(g3[:, None, :], (3, 128, D)))
    cw = np.asarray(inp["conv_w"], f)[0]
    cb = np.asarray(inp["conv_b"], f)[0]
    cp = np.stack([cw[0], cw[1], cw[2], cb])
    sh["convp"] = np.ascontiguousarray(cp.reshape(4, 2 * NFB, 128).transpose(2, 0, 1))
    ldt = np.asarray(inp["ssm_log_dt"], f)[0]
    are = np.asarray(inp["ssm_a_re"], f)[0]
    aim = np.asarray(inp["ssm_a_im"], f)[0]
    ldt_f = np.repeat(ldt, 64)
    tm = np.stack([ldt_f, are.reshape(-1), aim.reshape(-1)])
    sh["ssm_tm"] = np.ascontiguousarray(np.broadcast_to(tm[:, None, :], (3, 128, 8192)))
    def sm(a):
        return np.ascontiguousarray(a.reshape(64, 2, 64).transpose(1, 2, 0).reshape(128, 64))
    sh["ssm_sm"] = np.stack([sm(np.broadcast_to(ldt[:, None], (128, 64))), sm(are), sm(aim)]).astype(f)
    bre = np.asarray(inp["ssm_b_re"], f)[0]
    bim = np.asarray(inp["ssm_b_im"], f)[0]
    bblk = np.zeros((128, 16, 1024), f)
    for o in range(16):
        for gl in range(8):
            g = 8 * o + gl
            bblk[gl * 16:(gl + 1) * 16, o, gl * 64:(gl + 1) * 64] = bre[g].T
            bblk[gl * 16:(gl + 1) * 16, o, 512 + gl * 64:512 + (gl + 1) * 64] = bim[g].T
    sh["bblk"] = bblk
    cre = np.asarray(inp["ssm_c_re"], f)[0]
    cim = np.asarray(inp["ssm_c_im"], f)[0]
    cblk = np.zeros((2, 128, 64, 128), f)
    for pair in range(64):
        for g2 in range(2):
            g = 2 * pair + g2
            c0 = (pair % 4) * 32 + g2 * 16
            cblk[0, g2 * 64:(g2 + 1) * 64, pair, c0:c0 + 16] = cre[g].T
            cblk[1, g2 * 64:(g2 + 1) * 64, pair, c0:c0 + 16] = cim[g].T
    sh["cblk"] = cblk
    dsk = np.asarray(inp["ssm_d"], f)[0]
    sh["dsk"] = np.ascontiguousarray(dsk.reshape(16, 128).T)
    sh["sidx"] = np.arange(128, dtype=f).reshape(128, 1)
    sh["tidx"] = np.ascontiguousarray(np.broadcast_to(np.arange(128, dtype=f)[None, :], (128, 128)))
    kq = np.arange(128)
    tri = (kq[:, None] <= kq[None, :]).astype(f)
    identm = np.eye(128, dtype=f)
    maskc = np.where(kq[:, None] <= kq[None, :], 0.0, NEG).astype(f)
    maskp = np.where(kq[:, None] >= kq[None, :], 0.0, NEG).astype(f)
    prot = np.zeros((128, 128), f)
    for i in range(16):
        prot[i + 16, i] = -1.0
        prot[i, i + 16] = 1.0
    sh["ones"] = np.ones((128, 128), f)
    invf = np.zeros((128, 1), f)
    fr = (np.float32(500000.0) ** (-np.arange(0, 32, 2, dtype=f) / np.float32(32))).astype(f)
    invf[0:16, 0] = fr
    invf[16:32, 0] = fr
    sh["invf"] = invf
    for nm, bpp in W_BPP.items():
        arr = sh.pop(nm)
        for pc in range(-(-arr.shape[0] // bpp)):
            sh["%s_p%d" % (nm, pc)] = arr[pc * bpp:(pc + 1) * bpp]
    maps = []
    for c in range(NCORES):
        b, half = c // 2, c % 2
        m = dict(sh)
        xe = np.zeros((NTE, D), f)
        if half == 0:
            xe[OWN0:] = x[b, 0:2048]
        else:
            xe[:] = x[b]
        m["x"] = xe
        valid = np.full((128, 128), 1.0 if half == 1 else 0.0, f)
        m["cst"] = np.stack([tri, identm, maskp, maskc, valid, prot])
        pos = (np.arange(NTE, dtype=np.int64) + (half * 2048 - 2048)).astype(f)
        m["pos"] = np.ascontiguousarray(np.broadcast_to(pos[None, :], (128, NTE)))
        maps.append(m)
    return maps


_NC_CACHE = {}


def kernel(**inputs):
    maps = host_layout(inputs)
    if "nc" not in _NC_CACHE:
        _NC_CACHE["nc"] = build_program()
    nc = _NC_CACHE["nc"]
    res = run_bass_kernel_spmd(nc, maps, core_ids=list(range(NCORES)))
    out = np.zeros((4, SEQ, D), np.float32)
    for c in range(NCORES):
        b, half = c // 2, c % 2
        out[b, half * 2048:(half + 1) * 2048] = res.results[c]["out"]
    return out
```

```python
import math
from contextlib import ExitStack

import numpy as np
import ml_dtypes

import concourse.bass as bass
import concourse.mybir as mybir
from concourse.bass_utils import run_bass_kernel_spmd

F32 = mybir.dt.float32
BF16 = mybir.dt.bfloat16
AF = mybir.ActivationFunctionType
ALU = mybir.AluOpType

D = 4096
SEQ = 4096
NCORES = 8
NTE = 4096
OWN0 = 2048
EXT0 = 1920
NEXT = NTE - EXT0
DFF = 11008
NFB = DFF // 128
IN_COLS = 19456
RMS_EPS = 1e-5
NEG = -30000.0
SAME_ENG_SYNC = True
DMA_RING = 6
W_BPP = {"w_in": 8, "w_glu": 4, "w_ao": 8, "w_out": 4, "w_up": 4, "w_down": 1}


class Buf:
    __slots__ = ("name", "w", "r")

    def __init__(self, name=""):
        self.name = name
        self.w = None
        self.r = {}


class DBuf(Buf):
    pass


class T:
    __slots__ = ("h", "b")

    def __init__(self, h, name=""):
        self.h = h
        self.b = Buf(name)

    def __getitem__(self, k):
        return self.h[k]


class Sched:
    def __init__(self, nc):
        self.nc = nc
        self.eng = {}
        for n in ("tensor", "vector", "scalar", "gpsimd", "sync"):
            self.eng[n] = dict(e=getattr(nc, n), sem=nc.alloc_semaphore("sem_" + n), cnt=0, known={})
        self.dq = {}
        for q in ("sync", "gpsimd", "scalar"):
            self.dq[q] = dict(sems=[nc.alloc_semaphore("dq_%s_%d" % (q, i)) for i in range(DMA_RING)],
                              vals=[0] * DMA_RING, i=0)

    def _deps(self, reads, writes):
        deps = {}

        def add(d):
            if d is None:
                return
            k, sem, v = d
            if k not in deps or deps[k][1] < v:
                deps[k] = (sem, v)

        for b in reads:
            add(b.w)
        for b in writes:
            add(b.w)
            for d in b.r.values():
                add(d)
        return deps

    def _wait(self, en, deps):
        E = self.eng[en]
        for k, (sem, v) in deps.items():
            if k == en and (en == "tensor" or not SAME_ENG_SYNC):
                continue
            if E["known"].get(k, 0) >= v:
                continue
            E["e"].wait_ge(sem, v)
            E["known"][k] = v

    @staticmethod
    def _bufs(xs):
        return [x.b if isinstance(x, T) else x for x in xs if not isinstance(x, DBuf)]

    def op(self, en, fn, reads=(), writes=(), sig=True):
        reads = self._bufs(reads)
        writes = self._bufs(writes)
        E = self.eng[en]
        self._wait(en, self._deps(reads, writes))
        inst = fn()
        if sig:
            E["cnt"] += 1
            inst.then_inc(E["sem"], 1)
            tok = (en, E["sem"], E["cnt"])
        else:
            tok = (en, E["sem"], E["cnt"] + 1)
        for b in writes:
            b.w = tok
            b.r = {}
        for b in reads:
            b.r[en] = tok
        return inst

    def dma(self, q, out, in_, reads=(), writes=()):
        reads = self._bufs(reads)
        writes = self._bufs(writes)
        Q = self.dq[q]
        E = self.eng[q]
        self._wait(q, self._deps(reads, writes))
        slot = Q["i"] % DMA_RING
        Q["i"] += 1
        key = ("dma", q, slot)
        if Q["vals"][slot] > 0 and E["known"].get(key, 0) < Q["vals"][slot]:
            E["e"].wait_ge(Q["sems"][slot], Q["vals"][slot])
            E["known"][key] = Q["vals"][slot]
        inst = E["e"].dma_start(out=out, in_=in_)
        Q["vals"][slot] += 16
        inst.then_inc(Q["sems"][slot], 16)
        tok = (key, Q["sems"][slot], Q["vals"][slot])
        for b in writes:
            b.w = tok
            b.r = {}
        for b in reads:
            b.r[key] = tok
        return inst

    def barrier(self):
        toks = []
        for en, E in self.eng.items():
            if E["cnt"] > 0:
                toks.append((en, E["sem"], E["cnt"]))
        for q, Q in self.dq.items():
            for s in range(DMA_RING):
                if Q["vals"][s] > 0:
                    toks.append((("dma", q, s), Q["sems"][s], Q["vals"][s]))
        for en, E in self.eng.items():
            for k, sem, v in toks:
                if k == en:
                    continue
                if E["known"].get(k, 0) >= v:
                    continue
                E["e"].wait_ge(sem, v)
                E["known"][k] = v


_UID = [0]


class Ctx:
    def __init__(self, nc, S):
        self.nc = nc
        self.S = S
        self.es = ExitStack()
        self.n = 0

    def __enter__(self):
        self.es.__enter__()
        return self

    def __exit__(self, *a):
        self.S.barrier()
        return self.es.__exit__(*a)

    def sb(self, name, shape, dt):
        _UID[0] += 1
        return T(self.es.enter_context(self.nc.sbuf_tensor("%s_%d" % (name, _UID[0]), list(shape), dt)), name)

    def ps(self, name, shape, dt):
        _UID[0] += 1
        return T(self.es.enter_context(self.nc.psum_tensor("%s_%d" % (name, _UID[0]), list(shape), dt)), name)


class Rot:
    def __init__(self, items):
        self.items = items
        self.i = 0

    def next(self):
        x = self.items[self.i % len(self.items)]
        self.i += 1
        return x


def tok_groups(t0, n, g=512):
    out = []
    while n > 0:
        m = min(g, n)
        out.append((t0, m))
        t0 += m
        n -= m
    return out


def build_program(debug=False, nstages=99, s1mode=99, sel=None, ssm_mode=99):
    nc = bass.Bass("TRN2", target_bir_lowering=False)
    S = Sched(nc)
    V, A, G, PE = "vector", "scalar", "gpsimd", "tensor"

    def din(name, shape, dt=F32):
        return nc.dram_tensor(name, list(shape), dt, kind="ExternalInput").ap()

    def dscr(name, shape, dt):
        kind = "ExternalOutput" if debug else "Internal"
        return nc.dram_tensor(name, list(shape), dt, kind=kind).ap()

    class LazyW:
        def __init__(self, name, shape):
            self.name, self.shape, self.aps = name, shape, {}
            self.bpp = W_BPP[name]

        def __getitem__(self, k):
            if isinstance(k, tuple):
                blk, rest = k[0], k[1:]
            else:
                blk, rest = k, None
            pc = blk // self.bpp
            if pc not in self.aps:
                nb = min(self.bpp, self.shape[0] - pc * self.bpp)
                self.aps[pc] = din("%s_p%d" % (self.name, pc), [nb] + list(self.shape[1:]))
            ap = self.aps[pc][blk % self.bpp]
            return ap[rest] if rest is not None else ap

    x_d = din("x", [NTE, D])
    w_in = LazyW("w_in", [IN_COLS // 256, 128, 32 * 256])
    w_glu = LazyW("w_glu", [8, 128, 16 * 2 * 512])
    w_ao = LazyW("w_ao", [8, 128, 8 * 512])
    w_out = LazyW("w_out", [8, 128, 32 * 512])
    w_up = LazyW("w_up", [NFB // 2, 128, 32 * 2 * 256])
    w_down = LazyW("w_down", [8, 128, NFB * 512])
    g3_d = din("g3", [3, 128, D])
    convp_d = din("convp", [128, 4, 2 * NFB])
    ssm_tm_d = din("ssm_tm", [3, 128, 8192])
    ssm_sm_d = din("ssm_sm", [3, 128, 64])
    bblk_d = din("bblk", [128, 16, 1024])
    cblk_d = din("cblk", [2, 128, 64, 128])
    dsk_d = din("dsk", [128, 16])
    sidx_d = din("sidx", [128, 1])
    tidx_d = din("tidx", [128, 128])
    cst_d = din("cst", [6, 128, 128])
    ones_d = din("ones", [128, 128])
    pos_d = din("pos", [128, NTE])
    invf_d = din("invf", [128, 1])

    out_d = nc.dram_tensor("out", [2048, D], F32, kind="ExternalOutput").ap()

    h1T_d = dscr("h1T", [32, 128, NTE], BF16)
    uT_d = dscr("uT", [16, 128, NTE], BF16)
    qT_d = dscr("qT", [24, 128, NEXT], BF16)
    kT_d = dscr("kT", [24, 128, NTE], BF16)
    v3_d = dscr("vtok", [32, 128, 3072], BF16)
    v_d = v3_d.rearrange("a p c -> (a p) c")
    gT_d = dscr("gT", [64, 128, NEXT], BF16)
    yT_d = dscr("yT", [16, 128, NEXT], BF16)
    attnT_d = dscr("attnT", [8, 128, NEXT], BF16)
    mixT_d = dscr("mixT", [32, 128, NEXT], BF16)
    x1_3d = dscr("x1", [17, 128, D], F32)
    x1_d = x1_3d.rearrange("a p c -> (a p) c")
    h2T_d = dscr("h2T", [32, 128, NEXT], BF16)
    actT_d = dscr("actT", [NFB, 128, 2048], BF16)
    x2_3d = dscr("x2", [16, 128, D], F32)
    x2_d = x2_3d.rearrange("a p c -> (a p) c")

    DB = {n: DBuf(n) for n in ("h1T", "uT", "qT", "kT", "v", "gT", "yT", "attnT", "mixT", "x1", "h2T", "actT", "x2",
                              "out", "in")}

    top = ExitStack()

    def psb(name, shape, dt):
        return T(top.enter_context(nc.sbuf_tensor(name, list(shape), dt)), name)

    ident = psb("ident", [128, 128], BF16)
    S.dma(G, ident[:], cst_d[1], writes=[ident])
    _cm = {0: psb("cms0", [128, 64], F32), 1: psb("cms1", [128, 64], F32), "kk": psb("sc_kk", [128, 128], F32)}
    ssq = psb("ssq", [128, 16, 8], F32)

    def stage_normT(x_ap, chunks, gidx, hT_ap, tok_off, rd, wr):
        with Ctx(nc, S) as C:
            gbc = C.sb("gbc", [128, D], F32)
            S.dma("sync", gbc[:], g3_d[gidx], writes=[gbc])
            xt = Rot([C.sb("xt", [128, D], F32) for _ in range(2)])
            junk = C.sb("junk", [128, D], BF16)
            hb = Rot([C.sb("hb", [128, D], BF16) for _ in range(2)])
            hTs = Rot([C.sb("hTs", [128, 32, 512], BF16) for _ in range(2)])
            ss = Rot([C.sb("ss", [128, 1], F32) for _ in range(2)])
            PT = Rot([C.ps("PT", [128, 8, 128], BF16) for _ in range(4)])
            cpc = [0]
            items = []
            for g0 in range(0, len(chunks), 4):
                grp = chunks[g0:g0 + 4]
                st = hTs.next()
                for ci, c in enumerate(grp):
                    items.append((grp, ci, c, st))

            def partA(it):
                grp, ci, c, st = it
                X = xt.next()
                r0 = c * 128 - tok_off
                S.dma("sync", X[:], x_ap[r0:r0 + 128, :], reads=[rd], writes=[X])
                s_ = ss.next()
                S.op(A, lambda: nc.scalar.activation(out=junk[:], in_=X[:], func=AF.Square, accum_out=s_[:]),
                     reads=[X], writes=[junk, s_])
                S.op(V, lambda: nc.vector.tensor_scalar(out=s_[:], in0=s_[:], scalar1=1.0 / D, scalar2=RMS_EPS,
                                                        op0=ALU.mult, op1=ALU.add), reads=[s_], writes=[s_])
                S.op(A, lambda: nc.scalar.activation(out=s_[:], in_=s_[:], func=AF.Sqrt), reads=[s_], writes=[s_])
                S.op(V, lambda: nc.vector.reciprocal(out=s_[:], in_=s_[:]), reads=[s_], writes=[s_])
                H = hb.next()
                S.op(V, lambda: nc.vector.scalar_tensor_tensor(out=H[:], in0=X[:], scalar=s_[:], in1=gbc[:],
                                                               op0=ALU.mult, op1=ALU.mult),
                     reads=[X, s_, gbc], writes=[H])
                return H

            def partB(it, H):
                grp, ci, c, st = it
                for q in range(4):
                    P = PT.next()
                    for k8 in range(8):
                        k = q * 8 + k8
                        S.op(PE, lambda: nc.tensor.transpose(out=P[:, k8, :], in_=H[:, k * 128:(k + 1) * 128],
                                                             identity=ident[:]),
                             reads=[H, ident], writes=[P], sig=(k8 == 7))
                    dst = st[:, q * 8:(q + 1) * 8, ci * 128:(ci + 1) * 128]
                    if cpc[0] % 2 == 0:
                        S.op(V, lambda: nc.vector.tensor_copy(out=dst, in_=P[:]), reads=[P], writes=[st])
                    else:
                        S.op(A, lambda: nc.scalar.copy(out=dst, in_=P[:]), reads=[P], writes=[st])
                    cpc[0] += 1
                if ci == len(grp) - 1:
                    t0 = grp[0] * 128 - tok_off
                    n = len(grp) * 128
                    S.dma("sync", hT_ap.rearrange("k p t -> p k t")[:, :, t0:t0 + n], st[:, :, 0:n], reads=[st],
                          writes=[wr])

            Hs = {}
            for i in range(len(items) + 1):
                if i < len(items):
                    Hs[i] = partA(items[i])
                if i - 1 >= 0:
                    partB(items[i - 1], Hs.pop(i - 1))

    def load_aT(dst, src_ap, KT, t0, n, rd, step=8):
        for k0 in range(0, KT, step):
            k1 = min(KT, k0 + step)
            S.dma("sync", dst[:, k0:k1, 0:n], src_ap.rearrange("k p t -> p k t")[:, k0:k1, t0:t0 + n],
                  reads=[rd], writes=[dst])

    def wload(slot, src2d, nelem):
        for e0 in range(0, nelem, 8192):
            e1 = min(nelem, e0 + 8192)
            S.dma(G, slot[:, e0:e1], src2d[:, e0:e1], reads=[DB["in"]], writes=[slot])

    def mm_group(P_ap, Pt, KT, lhs_fn, rhs_fn, reads):
        for kt in range(KT):
            S.op(PE, lambda: nc.tensor.matmul(P_ap, lhsT=lhs_fn(kt), rhs=rhs_fn(kt), start=(kt == 0),
                                              stop=(kt == KT - 1)),
                 reads=reads, writes=[Pt], sig=(kt == KT - 1))

    def stage_inproj():
        with Ctx(nc, S) as C:
            cosT = C.sb("cosT", [128, NTE], F32)
            sinT = C.sb("sinT", [128, NTE], F32)
            prot = C.sb("prot", [128, 128], BF16)
            S.dma(G, prot[:], cst_d[5], writes=[prot])
            invf = C.sb("invf", [128, 1], F32)
            S.dma("sync", invf[:], invf_d, writes=[invf])
            with Ctx(nc, S) as C2:
                TWO_PI = 2.0 * math.pi
                C1 = 6.28125
                C2c = TWO_PI - C1
                MAGIC = 12582912.0
                for c0 in range(0, NTE, 1024):
                    ang = C2.sb("ang", [128, 1024], F32)
                    kk = C2.sb("kk", [128, 1024], F32)
                    rr = C2.sb("rr", [128, 1024], F32)
                    S.dma("sync", ang[:], pos_d[:, c0:c0 + 1024], writes=[ang])
                    S.op(V, lambda: nc.vector.tensor_scalar(out=ang[:], in0=ang[:], scalar1=invf[:], scalar2=None,
                                                            op0=ALU.mult), reads=[ang, invf], writes=[ang])
                    S.op(V, lambda: nc.vector.tensor_scalar(out=kk[:], in0=ang[:], scalar1=1.0 / TWO_PI, scalar2=MAGIC,
                                                            op0=ALU.mult, op1=ALU.add), reads=[ang], writes=[kk])
                    S.op(V, lambda: nc.vector.tensor_scalar(out=kk[:], in0=kk[:], scalar1=-MAGIC, scalar2=None,
                                                            op0=ALU.add), reads=[kk], writes=[kk])
                    S.op(V, lambda: nc.vector.scalar_tensor_tensor(out=rr[:], in0=kk[:], scalar=-C1, in1=ang[:],
                                                                   op0=ALU.mult, op1=ALU.add),
                         reads=[kk, ang], writes=[rr])
                    S.op(V, lambda: nc.vector.scalar_tensor_tensor(out=rr[:], in0=kk[:], scalar=-C2c, in1=rr[:],
                                                                   op0=ALU.mult, op1=ALU.add),
                         reads=[kk, rr], writes=[rr])
                    S.op(V, lambda: nc.vector.tensor_scalar(out=rr[:], in0=rr[:], scalar1=math.pi, scalar2=-math.pi,
                                                            op0=ALU.min, op1=ALU.max), reads=[rr], writes=[rr])
                    S.op(A, lambda: nc.scalar.activation(out=sinT[:, c0:c0 + 1024], in_=rr[:], func=AF.Sin),
                         reads=[rr], writes=[sinT])
                    S.op(A, lambda: nc.scalar.activation(out=kk[:], in_=rr[:], func=AF.Sin, scale=0.5),
                         reads=[rr], writes=[kk])
                    S.op(V, lambda: nc.vector.tensor_tensor(out=kk[:], in0=kk[:], in1=kk[:], op=ALU.mult),
                         reads=[kk], writes=[kk])
                    S.op(V, lambda: nc.vector.tensor_scalar(out=cosT[:, c0:c0 + 1024], in0=kk[:], scalar1=-2.0,
                                                            scalar2=1.0, op0=ALU.mult, op1=ALU.add),
                         reads=[kk], writes=[cosT])

            if s1mode == 0:
                return
            hT = C.sb("hT", [128, 32, 1152], BF16)
            CB = 256
            wsl = Rot([C.sb("wsl", [128, 32 * CB], BF16) for _ in range(2)])
            PS = Rot([C.ps("PS", [128, 512], F32) for _ in range(5)])
            PR = Rot([C.ps("PR", [128, 512], F32) for _ in range(2)])
            stg = Rot([C.sb("stg", [128, 512], BF16) for _ in range(3)])
            qraw = Rot([C.sb("qraw", [128, 512], BF16) for _ in range(2)])
            t1 = Rot([C.sb("t1", [128, 512], F32) for _ in range(2)])
            t2 = Rot([C.sb("t2", [128, 512], F32) for _ in range(2)])
            cnt = [0]

            fams = [("u", 0, 2048, 0)]
            for g in range(3):
                fams.append(("q%d" % g, 2048 + g * 1024, 1024, 15))
            kfirst = [14, 8, 0]
            for g in range(3):
                fams.append(("k%d" % g, 5120 + g * 1024, 1024, kfirst[g]))
            for g in range(3):
                fams.append(("v%d" % g, 8192 + g * 1024, 1024, kfirst[g]))
            fams.append(("gate", 11264, 8192, 15))

            sbs = [(0, 8), (8, 15), (15, 24), (24, 32)]
            if s1mode == 1:
                sbs, fams = sbs[:1], fams[:1]
            elif s1mode == 2:
                sbs = sbs[2:3]
                fams = [f_ for f_ in fams if f_[0][0] in "v"]
            elif s1mode == 3:
                sbs = sbs[2:3]
                fams = [f_ for f_ in fams if f_[0][0] in "q"]
            elif s1mode == 4:
                sbs = sbs[2:3]
                fams = [f_ for f_ in fams if f_[0][0] in "g"]
            elif s1mode == 8:
                sbs = sbs[3:4]
                fams = [("gate", 2048, 1024, 15)]
            elif s1mode == 9:
                sbs = sbs[3:4]
                fams = [f_ for f_ in fams if f_[0][0] in "q"]
            elif s1mode == 10:
                sbs = sbs[3:4]
                fams = [f_ for f_ in fams if f_[0][0] in "q"]
            elif s1mode == 11:
                sbs = sbs[3:4]
                fams = [f_ for f_ in fams if f_[0][0] in "v"]
            elif s1mode == 7:
                sbs = sbs[3:4]
                fams = [f_ for f_ in fams if f_[0][0] in "q"]
            elif s1mode == 5:
                sbs = sbs[2:3]
                fams = fams[:1]
            elif s1mode == 6:
                sbs = sbs[3:4]
                fams = [f_ for f_ in fams if f_[0][0] in "g"]
            for (sc0, sc1) in sbs:
                nt = (sc1 - sc0) * 128
                load_aT(hT, h1T_d, 32, sc0 * 128, nt, DB["h1T"])
                for (fname, fc0, fnc, first) in fams:
                    lo = max(sc0, first)
                    if lo >= sc1:
                        continue
                    tl0 = (lo - sc0) * 128
                    tn = (sc1 - lo) * 128
                    for cb0 in range(0, fnc, CB):
                        W = wsl.next()
                        wload(W, w_in[(fc0 + cb0) // CB], 32 * CB)
                        if fname[0] == "v":
                            g = int(fname[1])
                            for tc in range(tn // 128):
                                tl = tl0 + tc * 128
                                P = PS.next()
                                mm_group(P[:, 0:CB], P, 32, lambda kt: hT[:, kt, tl:tl + 128],
                                         lambda kt: W[:, kt * CB:(kt + 1) * CB], [hT, W])
                                sg = stg.next()
                                if cnt[0] % 2 == 0:
                                    S.op(V, lambda: nc.vector.tensor_copy(out=sg[:, 0:CB], in_=P[:, 0:CB]),
                                         reads=[P], writes=[sg])
                                else:
                                    S.op(A, lambda: nc.scalar.copy(out=sg[:, 0:CB], in_=P[:, 0:CB]),
                                         reads=[P], writes=[sg])
                                cnt[0] += 1
                                gi = sc0 * 128 + tl
                                if s1mode == 11:
                                    S.dma("sync", gT_d[g, :, 0:CB], sg[:, 0:CB], reads=[sg], writes=[DB["v"]])
                                else:
                                    S.dma("sync", v3_d[gi // 128, :, g * 1024 + cb0:g * 1024 + cb0 + CB], sg[:, 0:CB],
                                          reads=[sg], writes=[DB["v"]])
                            continue
                        for cc in range(CB // 128):
                            col = fc0 + cb0 + cc * 128
                            for (tg0, n) in tok_groups(tl0, tn):
                                P = PS.next()
                                mm_group(P[:, 0:n], P, 32, lambda kt: W[:, kt * CB + cc * 128:kt * CB + (cc + 1) * 128],
                                         lambda kt: hT[:, kt, tg0:tg0 + n], [hT, W])
                                gi = sc0 * 128 + tg0
                                sg = stg.next()
                                if fname == "u":
                                    if cnt[0] % 2 == 0:
                                        S.op(V, lambda: nc.vector.tensor_copy(out=sg[:, 0:n], in_=P[:, 0:n]),
                                             reads=[P], writes=[sg])
                                    else:
                                        S.op(A, lambda: nc.scalar.copy(out=sg[:, 0:n], in_=P[:, 0:n]),
                                             reads=[P], writes=[sg])
                                    cnt[0] += 1
                                    S.dma("sync", uT_d[col // 128, :, gi:gi + n], sg[:, 0:n], reads=[sg],
                                          writes=[DB["uT"]])
                                elif fname == "gate":
                                    S.op(A, lambda: nc.scalar.activation(out=sg[:, 0:n], in_=P[:, 0:n], func=AF.Sigmoid),
                                         reads=[P], writes=[sg])
                                    S.dma("sync", gT_d[((col - 11264) // 128) % 64, :, gi - EXT0:gi - EXT0 + n], sg[:, 0:n],
                                          reads=[sg], writes=[DB["gT"]])
                                else:
                                    qr = qraw.next()
                                    S.op(A, lambda: nc.scalar.copy(out=qr[:, 0:n], in_=P[:, 0:n]), reads=[P], writes=[qr])
                                    P2 = PR.next()
                                    if s1mode != 10:
                                        S.op(PE, lambda: nc.tensor.matmul(P2[:, 0:n], lhsT=prot[:], rhs=qr[:, 0:n],
                                                                          start=True, stop=True),
                                             reads=[prot, qr], writes=[P2])
                                    else:
                                        P2 = P
                                    a1 = t1.next()
                                    a2 = t2.next()
                                    S.op(V, lambda: nc.vector.tensor_tensor(out=a1[:, 0:n], in0=P[:, 0:n],
                                                                            in1=cosT[:, gi:gi + n], op=ALU.mult),
                                         reads=[P, cosT, qr], writes=[a1])
                                    S.op(V, lambda: nc.vector.tensor_tensor(out=a2[:, 0:n], in0=P2[:, 0:n],
                                                                            in1=sinT[:, gi:gi + n], op=ALU.mult),
                                         reads=[P2, sinT], writes=[a2])
                                    S.op(V, lambda: nc.vector.tensor_tensor(out=sg[:, 0:n], in0=a1[:, 0:n],
                                                                            in1=a2[:, 0:n], op=ALU.add),
                                         reads=[a1, a2], writes=[sg])
                                    if fname[0] == "q":
                                        hd = (col - 2048) // 128
                                        qdst = gT_d if s1mode == 9 else qT_d
                                        S.dma("sync", qdst[hd, :, gi - EXT0:gi - EXT0 + n], sg[:, 0:n], reads=[sg],
                                              writes=[DB["qT"]])
                                    else:
                                        hd = (col - 5120) // 128
                                        S.dma("sync", kT_d[hd, :, gi:gi + n], sg[:, 0:n], reads=[sg],
                                              writes=[DB["kT"]])

    def stage_ssm():
        with Ctx(nc, S) as C:
            tri = C.sb("tri", [128, 128], BF16)
            S.dma(G, tri[:], cst_d[0], writes=[tri])
            dsk = C.sb("dsk", [128, 16], F32)
            S.dma("sync", dsk[:], dsk_d, writes=[dsk])
            sidx = C.sb("sidx", [128, 1], F32)
            S.dma("sync", sidx[:], sidx_d, writes=[sidx])
            nsidx = C.sb("nsidx", [128, 1], F32)
            S.op(V, lambda: nc.vector.tensor_scalar(out=nsidx[:], in0=sidx[:], scalar1=-1.0, scalar2=None, op0=ALU.mult),
                 reads=[sidx], writes=[nsidx])
            pre_re = C.sb("pre_re", [128, 8192], BF16)
            pre_im = C.sb("pre_im", [128, 8192], BF16)
            post_re = C.sb("post_re", [128, 8192], BF16)
            post_im = C.sb("post_im", [128, 8192], BF16)
            l128_re = C.sb("l128_re", [128, 64], F32)
            l128_im = C.sb("l128_im", [128, 64], F32)
            TWO_PI = 2.0 * math.pi
            C1 = 6.28125
            C2c = TWO_PI - C1
            MAGIC = 12582912.0

            def sincos(C2, ang, n, sin_out, cos_out):
                kk = C2.sb("kk", [128, n], F32)
                S.op(V, lambda: nc.vector.tensor_scalar(out=kk[:], in0=ang[:], scalar1=1.0 / TWO_PI, scalar2=MAGIC,
                                                        op0=ALU.mult, op1=ALU.add), reads=[ang], writes=[kk])
                S.op(V, lambda: nc.vector.tensor_scalar(out=kk[:], in0=kk[:], scalar1=-MAGIC, scalar2=None,
                                                        op0=ALU.add), reads=[kk], writes=[kk])
                S.op(V, lambda: nc.vector.scalar_tensor_tensor(out=ang[:], in0=kk[:], scalar=-C1, in1=ang[:],
                                                               op0=ALU.mult, op1=ALU.add), reads=[kk, ang], writes=[ang])
                S.op(V, lambda: nc.vector.scalar_tensor_tensor(out=ang[:], in0=kk[:], scalar=-C2c, in1=ang[:],
                                                               op0=ALU.mult, op1=ALU.add), reads=[kk, ang], writes=[ang])
                S.op(V, lambda: nc.vector.tensor_scalar(out=ang[:], in0=ang[:], scalar1=math.pi, scalar2=-math.pi,
                                                        op0=ALU.min, op1=ALU.max), reads=[ang], writes=[ang])
                S.op(A, lambda: nc.scalar.activation(out=sin_out[:], in_=ang[:], func=AF.Sin), reads=[ang],
                     writes=[sin_out])
                S.op(A, lambda: nc.scalar.activation(out=kk[:], in_=ang[:], func=AF.Sin, scale=0.5), reads=[ang],
                     writes=[kk])
                S.op(V, lambda: nc.vector.tensor_tensor(out=kk[:], in0=kk[:], in1=kk[:], op=ALU.mult), reads=[kk],
                     writes=[kk])
                S.op(V, lambda: nc.vector.tensor_scalar(out=cos_out[:], in0=kk[:], scalar1=-2.0, scalar2=1.0,
                                                        op0=ALU.mult, op1=ALU.add), reads=[kk], writes=[cos_out])

            def tt(en, out, a, b, op, reads, writes):
                e = nc.vector if en == V else nc.gpsimd
                S.op(en, lambda: e.tensor_tensor(out=out, in0=a, in1=b, op=op), reads=reads, writes=writes)

            def cmul(C2, n, ar, ai, br, bi, outr, outi):
                m1 = C2.sb("m1", [128, n], F32)
                m2 = C2.sb("m2", [128, n], F32)
                tt(V, m1[:], ar[:], br[:], ALU.mult, [ar, br], [m1])
                tt(V, m2[:], ai[:], bi[:], ALU.mult, [ai, bi], [m2])
                tt(V, outr[:], m1[:], m2[:], ALU.subtract, [m1, m2], [outr])
                tt(V, m1[:], ar[:], bi[:], ALU.mult, [ar, bi], [m1])
                tt(V, m2[:], ai[:], br[:], ALU.mult, [ai, br], [m2])
                tt(V, outi[:], m1[:], m2[:], ALU.add, [m1, m2], [outi])

            def lam_f(C2, src_ap_fn, n):
                ldt = C2.sb("ldt", [128, n], F32)
                are = C2.sb("are", [128, n], F32)
                aim = C2.sb("aim", [128, n], F32)
                S.dma("sync", ldt[:], src_ap_fn(0), writes=[ldt])
                S.dma("sync", are[:], src_ap_fn(1), writes=[are])
                S.dma("sync", aim[:], src_ap_fn(2), writes=[aim])
                dt = C2.sb("dt", [128, n], F32)
                S.op(A, lambda: nc.scalar.activation(out=dt[:], in_=ldt[:], func=AF.Exp), reads=[ldt], writes=[dt])
                lnm = C2.sb("lnm", [128, n], F32)
                th = C2.sb("th", [128, n], F32)
                tt(V, lnm[:], dt[:], are[:], ALU.mult, [dt, are], [lnm])
                tt(V, th[:], dt[:], aim[:], ALU.mult, [dt, aim], [th])
                mag = C2.sb("mag", [128, n], F32)
                S.op(A, lambda: nc.scalar.activation(out=mag[:], in_=lnm[:], func=AF.Exp), reads=[lnm], writes=[mag])
                ang = C2.sb("ang", [128, n], F32)
                S.op(V, lambda: nc.vector.tensor_copy(out=ang[:], in_=th[:]), reads=[th], writes=[ang])
                sn = C2.sb("sn", [128, n], F32)
                cs = C2.sb("cs", [128, n], F32)
                sincos(C2, ang, n, sn, cs)
                lbr = C2.sb("lbr", [128, n], F32)
                lbi = C2.sb("lbi", [128, n], F32)
                tt(V, lbr[:], mag[:], cs[:], ALU.mult, [mag, cs], [lbr])
                tt(V, lbi[:], mag[:], sn[:], ALU.mult, [mag, sn], [lbi])
                S.op(V, lambda: nc.vector.tensor_scalar(out=lbr[:], in0=lbr[:], scalar1=-1.0, scalar2=None,
                                                        op0=ALU.add), reads=[lbr], writes=[lbr])
                den = C2.sb("den", [128, n], F32)
                d2 = C2.sb("d2", [128, n], F32)
                tt(V, den[:], are[:], are[:], ALU.mult, [are], [den])
                tt(V, d2[:], aim[:], aim[:], ALU.mult, [aim], [d2])
                tt(V, den[:], den[:], d2[:], ALU.add, [den, d2], [den])
                S.op(V, lambda: nc.vector.reciprocal(out=den[:], in_=den[:]), reads=[den], writes=[den])
                fr = C2.sb("fr", [128, n], F32)
                fi = C2.sb("fi", [128, n], F32)
                tt(V, fr[:], lbr[:], are[:], ALU.mult, [lbr, are], [fr])
                tt(V, d2[:], lbi[:], aim[:], ALU.mult, [lbi, aim], [d2])
                tt(V, fr[:], fr[:], d2[:], ALU.add, [fr, d2], [fr])
                tt(V, fr[:], fr[:], den[:], ALU.mult, [fr, den], [fr])
                tt(V, fi[:], lbi[:], are[:], ALU.mult, [lbi, are], [fi])
                tt(V, d2[:], lbr[:], aim[:], ALU.mult, [lbr, aim], [d2])
                tt(V, fi[:], fi[:], d2[:], ALU.subtract, [fi, d2], [fi])
                tt(V, fi[:], fi[:], den[:], ALU.mult, [fi, den], [fi])
                return lnm, th, fr, fi

            NS = 512
            for c0 in range(0, 8192, NS):
                with Ctx(nc, S) as C2:
                    lnm, th, fr, fi = lam_f(C2, lambda i: ssm_tm_d[i, :, c0:c0 + NS], NS)
                    mg = C2.sb("mg", [128, NS], F32)
                    S.op(A, lambda: nc.scalar.activation(out=mg[:], in_=lnm[:], func=AF.Exp, scale=nsidx[:]),
                         reads=[lnm, nsidx], writes=[mg])
                    ang = C2.sb("ang2", [128, NS], F32)
                    S.op(V, lambda: nc.vector.tensor_scalar(out=ang[:], in0=th[:], scalar1=nsidx[:], scalar2=None,
                                                            op0=ALU.mult), reads=[th, nsidx], writes=[ang])
                    sn = C2.sb("sn2", [128, NS], F32)
                    cs = C2.sb("cs2", [128, NS], F32)
                    sincos(C2, ang, NS, sn, cs)
                    er = C2.sb("er", [128, NS], F32)
                    ei = C2.sb("ei", [128, NS], F32)
                    tt(V, er[:], mg[:], cs[:], ALU.mult, [mg, cs], [er])
                    tt(V, ei[:], mg[:], sn[:], ALU.mult, [mg, sn], [ei])
                    outr = C2.sb("outr", [128, NS], F32)
                    outi = C2.sb("outi", [128, NS], F32)
                    cmul(C2, NS, er, ei, fr, fi, outr, outi)
                    S.op(V, lambda: nc.vector.tensor_copy(out=pre_re[:, c0:c0 + NS], in_=outr[:]), reads=[outr],
                         writes=[pre_re])
                    S.op(V, lambda: nc.vector.tensor_copy(out=pre_im[:, c0:c0 + NS], in_=outi[:]), reads=[outi],
                         writes=[pre_im])
            if ssm_mode == 0:
                return
            with Ctx(nc, S) as C2:
                lnm, th, fr, fi = lam_f(C2, lambda i: ssm_sm_d[i], 64)
                tidx = C2.sb("tidx", [128, 128], F32)
                S.dma("sync", tidx[:], tidx_d, writes=[tidx])
                mg = C2.sb("mg3", [128, 128], F32)
                ang = C2.sb("ang3", [128, 128], F32)
                sn = C2.sb("sn3", [128, 128], F32)
                cs = C2.sb("cs3", [128, 128], F32)
                for j in range(64):
                    S.op(A, lambda: nc.scalar.activation(out=mg[:], in_=tidx[:], func=AF.Exp, scale=lnm[:, j:j + 1]),
                         reads=[tidx, lnm], writes=[mg])
                    S.op(V, lambda: nc.vector.tensor_scalar(out=ang[:], in0=tidx[:], scalar1=th[:, j:j + 1],
                                                            scalar2=None, op0=ALU.mult), reads=[tidx, th], writes=[ang])
                    sincos_reuse(C2, ang, sn, cs)
                    tt(V, post_re[:, j * 128:(j + 1) * 128], mg[:], cs[:], ALU.mult, [mg, cs], [post_re])
                    tt(V, post_im[:, j * 128:(j + 1) * 128], mg[:], sn[:], ALU.mult, [mg, sn], [post_im])
                mg2 = C2.sb("mg4", [128, 64], F32)
                S.op(A, lambda: nc.scalar.activation(out=mg2[:], in_=lnm[:], func=AF.Exp, scale=128.0), reads=[lnm],
                     writes=[mg2])
                ang2 = C2.sb("ang4", [128, 64], F32)
                S.op(V, lambda: nc.vector.tensor_scalar(out=ang2[:], in0=th[:], scalar1=128.0, scalar2=None,
                                                        op0=ALU.mult), reads=[th], writes=[ang2])
                sn2 = C2.sb("sn4", [128, 64], F32)
                cs2 = C2.sb("cs4", [128, 64], F32)
                sincos(C2, ang2, 64, sn2, cs2)
                tt(V, l128_re[:], mg2[:], cs2[:], ALU.mult, [mg2, cs2], [l128_re])
                tt(V, l128_im[:], mg2[:], sn2[:], ALU.mult, [mg2, sn2], [l128_im])

            if ssm_mode == 1:
                return
            bblk = C.sb("bblk", [128, 16, 1024], BF16)
            for o8 in range(0, 16, 8):
                S.dma(G, bblk[:, o8:o8 + 8, :], bblk_d[:, o8:o8 + 8, :], writes=[bblk])
            cre = C.sb("cre", [128, 64, 128], BF16)
            cimn = C.sb("cimn", [128, 64, 128], BF16)
            S.dma(G, cre[:], cblk_d[0], writes=[cre])
            S.dma(G, cimn[:], cblk_d[1], writes=[cimn])
            S.op(V, lambda: nc.vector.tensor_scalar(out=cimn[:], in0=cimn[:], scalar1=-1.0, scalar2=None, op0=ALU.mult),
                 reads=[cimn], writes=[cimn])
            car_re = C.sb("car_re", [128, 64], F32)
            car_im = C.sb("car_im", [128, 64], F32)
            tot_re = C.sb("tot_re", [128, 64], F32)
            tot_im = C.sb("tot_im", [128, 64], F32)
            S.op(V, lambda: nc.vector.memset(car_re[:], 0.0), writes=[car_re])
            S.op(V, lambda: nc.vector.memset(car_im[:], 0.0), writes=[car_im])
            uTs = Rot([C.sb("uTs", [128, 16, 128], BF16) for _ in range(2)])
            m = [Rot([C.sb("m%d" % i, [128, 512], BF16) for _ in range(2)]) for i in range(4)]
            pm = [Rot([C.sb("pm%d" % i, [128, 512], BF16) for _ in range(2)]) for i in range(4)]
            crr = Rot([C.sb("crr", [128, 512], BF16) for _ in range(2)])
            cir = Rot([C.sb("cir", [128, 512], BF16) for _ in range(2)])
            trin = C.sb("trin", [128, 128], BF16)
            S.op(V, lambda: nc.vector.tensor_scalar(out=trin[:], in0=tri[:], scalar1=-1.0, scalar2=None, op0=ALU.mult),
                 reads=[tri], writes=[trin])
            vt_re = Rot([C.sb("vt_re", [128, 512], BF16) for _ in range(2)])
            vt_im = Rot([C.sb("vt_im", [128, 512], BF16) for _ in range(2)])
            xr = Rot([C.sb("xr", [128, 512], BF16) for _ in range(2)])
            xi = Rot([C.sb("xi", [128, 512], BF16) for _ in range(2)])
            ystg = Rot([C.sb("ystg", [128, 2048], BF16) for _ in range(2)])
            Y0 = C.sb("Y0", [128, 2048], F32)
            Y1 = C.sb("Y1", [128, 2048], F32)
            PBU = Rot([C.ps("PBU", [128, 1024], F32) for _ in range(1)])
            PCR = Rot([C.ps("PCR", [128, 4, 128], F32) for _ in range(2)])
            PCI = Rot([C.ps("PCI", [128, 4, 128], F32) for _ in range(2)])
            PY = Rot([C.ps("PY", [128, 512], F32) for _ in range(2)])
            U = None
            if ssm_mode == 2:
                return
            for c in ([0] if ssm_mode == 3 else ([15] if ssm_mode == 4 else range(32))):
                U = uTs.next()
                S.dma("sync", U[:], uT_d.rearrange("k p t -> p k t")[:, :, c * 128:c * 128 + 128],
                      reads=[DB["uT"]], writes=[U])
                cl = 0
                full = c >= 15
                ys = ystg.next() if full else None
                st_ = {}
                tk_ = {}

                def stA(o):
                    Pb = PBU.next()
                    for hf in range(2):
                        S.op(PE, lambda: nc.tensor.matmul(Pb[:, hf * 512:(hf + 1) * 512], lhsT=U[:, o, cl:cl + 128],
                                                          rhs=bblk[:, o, hf * 512:(hf + 1) * 512], start=True, stop=True),
                             reads=[U, bblk], writes=[Pb], sig=(hf == 1))
                    bre = Pb[:, 0:512]
                    bim = Pb[:, 512:1024]
                    er = pre_re[:, o * 512:(o + 1) * 512]
                    ei = pre_im[:, o * 512:(o + 1) * 512]
                    m0, m1, m2, m3 = (mm_.next() for mm_ in m)
                    tt(V, m0[:], bre, er, ALU.mult, [Pb, pre_re], [m0])
                    tt(V, m1[:], bim, ei, ALU.mult, [Pb, pre_im], [m1])
                    tt(V, m2[:], bim, er, ALU.mult, [Pb, pre_re], [m2])
                    tt(V, m3[:], bre, ei, ALU.mult, [Pb, pre_im], [m3])
                    vr = vt_re.next()
                    vi = vt_im.next()
                    tt(V, vr[:], m0[:], m1[:], ALU.subtract, [m0, m1], [vr])
                    tt(V, vi[:], m2[:], m3[:], ALU.add, [m2, m3], [vi])
                    Pr = PCR.next()
                    Pi = PCI.next()
                    c0_ = 0 if full else 126
                    ncol = 128 if full else 2
                    for j4 in range(4):
                        blk = slice(j4 * 128, (j4 + 1) * 128)
                        S.op(PE, lambda: nc.tensor.matmul(Pr[:, j4, 0:ncol], lhsT=vr[:, blk], rhs=tri[:, c0_:128],
                                                          start=True, stop=True), reads=[vr, tri], writes=[Pr],
                             sig=(j4 == 3))
                    for j4 in range(4):
                        blk = slice(j4 * 128, (j4 + 1) * 128)
                        S.op(PE, lambda: nc.tensor.matmul(Pi[:, j4, 0:ncol], lhsT=vi[:, blk], rhs=tri[:, c0_:128],
                                                          start=True, stop=True), reads=[vi, tri], writes=[Pi],
                             sig=(j4 == 3))
                    st_[o] = (Pr, Pi)

                def stT(o):
                    Pr, Pi = st_[o]
                    lc = 127 if full else 1
                    tk_[o] = Buf("tk")
                    tt(V, tot_re[:, o * 4:(o + 1) * 4], Pr[:, :, lc], car_re[:, o * 4:(o + 1) * 4], ALU.add,
                       [Pr, car_re], [tot_re, tk_[o]])
                    tt(V, tot_im[:, o * 4:(o + 1) * 4], Pi[:, :, lc], car_im[:, o * 4:(o + 1) * 4], ALU.add,
                       [Pi, car_im], [tot_im, tk_[o]])

                def stB(o):
                    Pr, Pi = st_[o]
                    A0, A1, A2, A3 = (p_.next() for p_ in pm)
                    CR = crr.next()
                    CI = cir.next()
                    for j4 in range(4):
                        j = o * 4 + j4
                        blk = slice(j4 * 128, (j4 + 1) * 128)
                        S.op(A, lambda: nc.scalar.activation(out=CR[:, blk], in_=Pr[:, j4, :], func=AF.Identity,
                                                             bias=car_re[:, j:j + 1]), reads=[Pr, car_re, tk_[o]],
                             writes=[CR])
                        S.op(A, lambda: nc.scalar.activation(out=CI[:, blk], in_=Pi[:, j4, :], func=AF.Identity,
                                                             bias=car_im[:, j:j + 1]), reads=[Pi, car_im, tk_[o]],
                             writes=[CI])
                    osl = slice(o * 512, (o + 1) * 512)
                    tt(V, A0[:], CR[:], post_re[:, osl], ALU.mult, [CR, post_re], [A0])
                    tt(V, A1[:], CI[:], post_im[:, osl], ALU.mult, [CI, post_im], [A1])
                    tt(V, A2[:], CR[:], post_im[:, osl], ALU.mult, [CR, post_im], [A2])
                    tt(V, A3[:], CI[:], post_re[:, osl], ALU.mult, [CI, post_re], [A3])
                    X_r = xr.next()
                    X_i = xi.next()
                    tt(V, X_r[:], A0[:], A1[:], ALU.subtract, [A0, A1], [X_r])
                    tt(V, X_i[:], A2[:], A3[:], ALU.add, [A2, A3], [X_i])
                    Py = PY.next()
                    for j4 in range(4):
                        j = o * 4 + j4
                        blk = slice(j4 * 128, (j4 + 1) * 128)
                        S.op(PE, lambda: nc.tensor.matmul(Py[:, 0:128], lhsT=cre[:, j, :], rhs=X_r[:, blk],
                                                          start=(j4 == 0), stop=False), reads=[cre, X_r], writes=[Py],
                             sig=False)
                        S.op(PE, lambda: nc.tensor.matmul(Py[:, 0:128], lhsT=cimn[:, j, :], rhs=X_i[:, blk],
                                                          start=False, stop=(j4 == 3)), reads=[cimn, X_i], writes=[Py],
                             sig=(j4 == 3))
                    st_[o] = Py

                def stC(o):
                    Py = st_[o]
                    S.op(V, lambda: nc.vector.scalar_tensor_tensor(out=Y0[:, o * 128:(o + 1) * 128],
                                                                   in0=U[:, o, cl:cl + 128], scalar=dsk[:, o:o + 1],
                                                                   in1=Py[:, 0:128], op0=ALU.mult, op1=ALU.add),
                         reads=[U, dsk, Py], writes=[Y0])

                if not full:
                    for step in range(16 + 1):
                        if step < 16:
                            stA(step)
                        if step - 1 >= 0:
                            stT(step - 1)
                else:
                    for step in range(16 + 2):
                        if step < 16:
                            stA(step)
                        if 0 <= step - 1 < 16:
                            stB(step - 1)
                        if step < 16:
                            stT(step)
                        if 0 <= step - 2 < 16:
                            stC(step - 2)
                    S.op(A, lambda: nc.scalar.activation(out=Y1[:], in_=Y0[:], func=AF.Square), reads=[Y0], writes=[Y1])
                    S.op(V, lambda: nc.vector.tensor_scalar(out=Y1[:], in0=Y1[:], scalar1=0.044715, scalar2=1.0,
                                                            op0=ALU.mult, op1=ALU.add), reads=[Y1], writes=[Y1])
                    tt(V, Y1[:], Y1[:], Y0[:], ALU.mult, [Y1, Y0], [Y1])
                    S.op(A, lambda: nc.scalar.activation(out=Y1[:], in_=Y1[:], func=AF.Sigmoid, scale=1.5957691216),
                         reads=[Y1], writes=[Y1])
                    tt(V, ys[:], Y0[:], Y1[:], ALU.mult, [Y0, Y1], [ys])
                cmul_small(tot_re, tot_im, l128_re, l128_im, car_re, car_im)
                if full:
                    e0 = c * 128 - EXT0
                    S.dma("sync", yT_d.rearrange("k p t -> p k t")[:, :, e0:e0 + 128],
                          ys[:].rearrange("p (k t) -> p k t", k=16), reads=[ys], writes=[DB["yT"]])

    def cmul_small(ar, ai, br, bi, outr, outi):
        m1, m2 = _cm[0], _cm[1]
        S.op(V, lambda: nc.vector.tensor_tensor(out=m1[:], in0=ar[:], in1=br[:], op=ALU.mult), reads=[ar, br], writes=[m1])
        S.op(V, lambda: nc.vector.tensor_tensor(out=m2[:], in0=ai[:], in1=bi[:], op=ALU.mult), reads=[ai, bi], writes=[m2])
        S.op(V, lambda: nc.vector.tensor_tensor(out=outr[:], in0=m1[:], in1=m2[:], op=ALU.subtract), reads=[m1, m2],
             writes=[outr])
        S.op(V, lambda: nc.vector.tensor_tensor(out=m1[:], in0=ar[:], in1=bi[:], op=ALU.mult), reads=[ar, bi], writes=[m1])
        S.op(V, lambda: nc.vector.tensor_tensor(out=m2[:], in0=ai[:], in1=br[:], op=ALU.mult), reads=[ai, br], writes=[m2])
        S.op(V, lambda: nc.vector.tensor_tensor(out=outi[:], in0=m1[:], in1=m2[:], op=ALU.add), reads=[m1, m2],
             writes=[outi])

    def sincos_reuse(C2, ang, sn, cs):
        kk = _cm["kk"]
        TWO_PI = 2.0 * math.pi
        C1 = 6.28125
        C2c = TWO_PI - C1
        MAGIC = 12582912.0
        S.op(V, lambda: nc.vector.tensor_scalar(out=kk[:], in0=ang[:], scalar1=1.0 / TWO_PI, scalar2=MAGIC,
                                                op0=ALU.mult, op1=ALU.add), reads=[ang], writes=[kk])
        S.op(V, lambda: nc.vector.tensor_scalar(out=kk[:], in0=kk[:], scalar1=-MAGIC, scalar2=None, op0=ALU.add),
             reads=[kk], writes=[kk])
        S.op(V, lambda: nc.vector.scalar_tensor_tensor(out=ang[:], in0=kk[:], scalar=-C1, in1=ang[:], op0=ALU.mult,
                                                       op1=ALU.add), reads=[kk, ang], writes=[ang])
        S.op(V, lambda: nc.vector.scalar_tensor_tensor(out=ang[:], in0=kk[:], scalar=-C2c, in1=ang[:], op0=ALU.mult,
                                                       op1=ALU.add), reads=[kk, ang], writes=[ang])
        S.op(V, lambda: nc.vector.tensor_scalar(out=ang[:], in0=ang[:], scalar1=math.pi, scalar2=-math.pi,
                                                op0=ALU.min, op1=ALU.max), reads=[ang], writes=[ang])
        S.op(A, lambda: nc.scalar.activation(out=sn[:], in_=ang[:], func=AF.Sin), reads=[ang], writes=[sn])
        S.op(A, lambda: nc.scalar.activation(out=kk[:], in_=ang[:], func=AF.Sin, scale=0.5), reads=[ang], writes=[kk])
        S.op(V, lambda: nc.vector.tensor_tensor(out=kk[:], in0=kk[:], in1=kk[:], op=ALU.mult), reads=[kk], writes=[kk])
        S.op(V, lambda: nc.vector.tensor_scalar(out=cs[:], in0=kk[:], scalar1=-2.0, scalar2=1.0, op0=ALU.mult,
                                                op1=ALU.add), reads=[kk], writes=[cs])

    def stage_attn():
        with Ctx(nc, S) as C:
            maskp = C.sb("maskp", [128, 128], BF16)
            maskc = C.sb("maskc", [128, 128], BF16)
            validh = C.sb("validh", [128, 128], BF16)
            ones = C.sb("ones", [128, 128], BF16)
            S.dma(G, maskp[:], cst_d[2], writes=[maskp])
            S.dma(G, maskc[:], cst_d[3], writes=[maskc])
            S.dma(G, validh[:], cst_d[4], writes=[validh])
            S.dma(G, ones[:], ones_d, writes=[ones])
            qs = Rot([C.sb("qs", [128, NEXT], BF16) for _ in range(2)])
            ks = Rot([C.sb("ks", [128, NTE], BF16) for _ in range(2)])
            vs = Rot([C.sb("vs", [128, 18, 128], BF16) for _ in range(3)])
            num = C.sb("num", [128, NEXT], F32)
            den = C.sb("den", [128, NEXT], F32)
            ob = Rot([C.sb("ob", [128, NEXT], BF16) for _ in range(2)])
            es = Rot([C.sb("es", [128, 256], BF16) for _ in range(5)])
            PSs = Rot([C.ps("PSs", [128, 512], F32) for _ in range(4)])
            PO = Rot([C.ps("PO", [128, 512], F32) for _ in range(2)])
            PD = Rot([C.ps("PD", [128, 512], F32) for _ in range(2)])
            scale = 1.0 / math.sqrt(128.0)
            dil = [1, 4, 16]
            pend = []
            for h in range(8):
                for g in range(3):
                    d = dil[g]
                    hd = g * 8 + h
                    Q = qs.next()
                    K = ks.next()
                    S.dma("sync", Q[:], qT_d[hd], reads=[DB["qT"]], writes=[Q])
                    S.dma("sync", K[:], kT_d[hd], reads=[DB["kT"]], writes=[K])
                    nclass = NTE // d
                    nblk = nclass // 128
                    m_lo = EXT0 // d
                    nb_lo = m_lo // 128
                    kb_lo = max(0, nb_lo - 1)
                    nkb = nblk - kb_lo
                    for r in range(d):
                        Vt = vs.next()
                        vsrc = v_d.rearrange("(kb j dd) c -> j kb dd c", j=128, dd=d)[:, kb_lo:nblk, r,
                                                                                        hd * 128:(hd + 1) * 128]
                        S.dma("sync", Vt[:, 0:nkb, :], vsrc, reads=[DB["v"]], writes=[Vt])
                        for nb in range(nb_lo, nblk):
                            qi_min = max(0, -(-(EXT0 - r) // d) - 128 * nb)
                            if qi_min >= 128:
                                continue
                            q0 = (128 * nb) * d + r - EXT0
                            qa = q0 + qi_min * d
                            nq = 128 - qi_min
                            q_ap = Q[:, qa:qa + (nq - 1) * d + 1:d]
                            Ps = PSs.next()
                            blocks = []
                            if nb - 1 >= kb_lo and nb - 1 >= 0:
                                blocks.append((nb - 1, maskp, 0))
                            blocks.append((nb, maskc, 1))
                            E = es.next()
                            for (kb, msk, slot) in blocks:
                                k0 = (128 * kb) * d + r
                                k_ap = K[:, k0:k0 + 127 * d + 1:d]
                                S.op(PE, lambda: nc.tensor.matmul(Ps[:, slot * 128:slot * 128 + nq], lhsT=k_ap, rhs=q_ap,
                                                                  start=True, stop=False), reads=[K, Q], writes=[Ps],
                                     sig=False)
                                S.op(PE, lambda: nc.tensor.matmul(Ps[:, slot * 128:slot * 128 + nq], lhsT=ident[:],
                                                                  rhs=msk[:, qi_min:128], start=False, stop=True),
                                     reads=[ident, msk], writes=[Ps], sig=True)
                                S.op(A, lambda: nc.scalar.activation(out=E[:, slot * 128:slot * 128 + nq],
                                                                     in_=Ps[:, slot * 128:slot * 128 + nq], func=AF.Exp,
                                                                     scale=scale), reads=[Ps], writes=[E])
                            def P_fn(blocks=blocks, E=E, Vt=Vt, nq=nq, qa=qa, d=d, g=g, kb_lo=kb_lo, r=r):
                                Po = PO.next()
                                Pd = PD.next()
                                for bi, (kb, msk, slot) in enumerate(blocks):
                                    S.op(PE, lambda: nc.tensor.matmul(Po[:, 0:nq], lhsT=Vt[:, kb - kb_lo, :],
                                                                      rhs=E[:, slot * 128:slot * 128 + nq], start=(bi == 0),
                                                                      stop=(bi == len(blocks) - 1)), reads=[Vt, E],
                                         writes=[Po], sig=(bi == len(blocks) - 1))
                                for bi, (kb, msk, slot) in enumerate(blocks):
                                    key_halo = ((128 * kb) * d + r) < OWN0
                                    vm = validh if key_halo else ones
                                    S.op(PE, lambda: nc.tensor.matmul(Pd[:, 0:nq], lhsT=vm[:],
                                                                      rhs=E[:, slot * 128:slot * 128 + nq], start=(bi == 0),
                                                                      stop=(bi == len(blocks) - 1)), reads=[vm, E],
                                         writes=[Pd], sig=(bi == len(blocks) - 1))
                                n_ap = num[:, qa:qa + (nq - 1) * d + 1:d]
                                d_ap = den[:, qa:qa + (nq - 1) * d + 1:d]
                                if g == 0:
                                    S.op(V, lambda: nc.vector.tensor_copy(out=n_ap, in_=Po[:, 0:nq]), reads=[Po], writes=[num])
                                    S.op(V, lambda: nc.vector.tensor_copy(out=d_ap, in_=Pd[:, 0:nq]), reads=[Pd], writes=[den])
                                else:
                                    S.op(V, lambda: nc.vector.tensor_tensor(out=n_ap, in0=Po[:, 0:nq], in1=n_ap, op=ALU.add),
                                         reads=[Po, num], writes=[num])
                                    S.op(V, lambda: nc.vector.tensor_tensor(out=d_ap, in0=Pd[:, 0:nq], in1=d_ap, op=ALU.add),
                                         reads=[Pd, den], writes=[den])

                            pend.append(P_fn)
                            if len(pend) > 2:
                                pend.pop(0)()
                while pend:
                    pend.pop(0)()
                S.op(V, lambda: nc.vector.tensor_scalar(out=den[:], in0=den[:], scalar1=1e-30, scalar2=None, op0=ALU.max),
                     reads=[den], writes=[den])
                S.op(V, lambda: nc.vector.reciprocal(out=den[:], in_=den[:]), reads=[den], writes=[den])
                O = ob.next()
                S.op(V, lambda: nc.vector.tensor_tensor(out=O[:], in0=num[:], in1=den[:], op=ALU.mult), reads=[num, den],
                     writes=[O])
                S.dma("sync", attnT_d[h], O[:], reads=[O], writes=[DB["attnT"]])

    def stage_mix():
        with Ctx(nc, S) as C:
            yTs = C.sb("yTs", [128, 16, 1152], BF16)
            aTs = C.sb("aTs", [128, 8, 1152], BF16)
            wg = Rot([C.sb("wg", [128, 16 * 2 * 512], BF16) for _ in range(2)])
            wa = Rot([C.sb("wa", [128, 8 * 512], BF16) for _ in range(2)])
            gs = Rot([C.sb("gs", [128, 1152], BF16) for _ in range(2)])
            ga = Rot([C.sb("ga", [128, 1152], BF16) for _ in range(2)])
            sgm = Rot([C.sb("sgm", [128, 512], F32) for _ in range(2)])
            ta = Rot([C.sb("ta", [128, 512], F32) for _ in range(2)])
            tb = Rot([C.sb("tb", [128, 512], F32) for _ in range(2)])
            mst = Rot([C.sb("mst", [128, 512], BF16) for _ in range(3)])
            PA = Rot([C.ps("PA", [128, 512], F32) for _ in range(2)])
            PB = Rot([C.ps("PB", [128, 512], F32) for _ in range(2)])
            PC = Rot([C.ps("PC", [128, 512], F32) for _ in range(2)])
            for (e0, nt) in ((0, 1152), (1152, 1024)):
                load_aT(yTs, yT_d, 16, e0, nt, DB["yT"])
                load_aT(aTs, attnT_d, 8, e0, nt, DB["attnT"])
                for jb in range(8):
                    Wg = wg.next()
                    Wa = wa.next()
                    wload(Wg, w_glu[jb], 16 * 2 * 512)
                    wload(Wa, w_ao[jb], 8 * 512)
                    for jj in range(4):
                        j = jb * 4 + jj
                        Gs = gs.next()
                        Ga = ga.next()
                        S.dma("sync", Gs[:, 0:nt], gT_d[j, :, e0:e0 + nt], reads=[DB["gT"]], writes=[Gs])
                        S.dma("sync", Ga[:, 0:nt], gT_d[32 + j, :, e0:e0 + nt], reads=[DB["gT"]], writes=[Ga])
                        for (t0, n) in tok_groups(0, nt):
                            Pa = PA.next()
                            Pb = PB.next()
                            Pc = PC.next()
                            mm_group(Pa[:, 0:n], Pa, 16, lambda kt: Wg[:, kt * 1024 + jj * 128:kt * 1024 + (jj + 1) * 128],
                                     lambda kt: yTs[:, kt, t0:t0 + n], [Wg, yTs])
                            mm_group(Pb[:, 0:n], Pb, 16, lambda kt: Wg[:, kt * 1024 + 512 + jj * 128:kt * 1024 + 512 + (jj + 1) * 128],
                                     lambda kt: yTs[:, kt, t0:t0 + n], [Wg, yTs])
                            mm_group(Pc[:, 0:n], Pc, 8, lambda kt: Wa[:, kt * 512 + jj * 128:kt * 512 + (jj + 1) * 128],
                                     lambda kt: aTs[:, kt, t0:t0 + n], [Wa, aTs])
                            sg_ = sgm.next()
                            a_ = ta.next()
                            b_ = tb.next()
                            o_ = mst.next()
                            S.op(A, lambda: nc.scalar.activation(out=sg_[:, 0:n], in_=Pb[:, 0:n], func=AF.Sigmoid),
                                 reads=[Pb], writes=[sg_])
                            S.op(V, lambda: nc.vector.tensor_tensor(out=a_[:, 0:n], in0=Pa[:, 0:n], in1=sg_[:, 0:n],
                                                                    op=ALU.mult), reads=[Pa, sg_], writes=[a_])
                            S.op(V, lambda: nc.vector.tensor_tensor(out=a_[:, 0:n], in0=a_[:, 0:n], in1=Gs[:, t0:t0 + n],
                                                                    op=ALU.mult), reads=[a_, Gs], writes=[a_])
                            S.op(V, lambda: nc.vector.tensor_tensor(out=b_[:, 0:n], in0=Pc[:, 0:n], in1=Ga[:, t0:t0 + n],
                                                                    op=ALU.mult), reads=[Pc, Ga], writes=[b_])
                            S.op(V, lambda: nc.vector.tensor_tensor(out=o_[:, 0:n], in0=a_[:, 0:n], in1=b_[:, 0:n],
                                                                    op=ALU.add), reads=[a_, b_], writes=[o_])
                            S.dma("sync", mixT_d[j, :, e0 + t0:e0 + t0 + n], o_[:, 0:n], reads=[o_],
                                  writes=[DB["mixT"]])

    def stage_wout():
        with Ctx(nc, S) as C:
            mT = C.sb("mT", [128, 32, 1152], BF16)
            CB = 512
            wsl = Rot([C.sb("wsl2", [128, 32 * CB], BF16) for _ in range(2)])
            xin = Rot([C.sb("xin", [128, CB], F32) for _ in range(3)])
            xo = Rot([C.sb("xo", [128, CB], F32) for _ in range(3)])
            PS = Rot([C.ps("PS2", [128, 512], F32) for _ in range(4)])
            for (e0, nt) in ((0, 1152), (1152, 1024)):
                load_aT(mT, mixT_d, 32, e0, nt, DB["mixT"])
                for cb in range(D // CB):
                    W = wsl.next()
                    wload(W, w_out[cb], 32 * CB)
                    for tc in range(nt // 128):
                        tl = tc * 128
                        Xi = xin.next()
                        gi = EXT0 + e0 + tl
                        S.dma("sync", Xi[:], x_d[gi:gi + 128, cb * CB:(cb + 1) * CB], reads=[DB["in"]], writes=[Xi])
                        P = PS.next()
                        mm_group(P[:], P, 32, lambda kt: mT[:, kt, tl:tl + 128], lambda kt: W[:, kt * CB:(kt + 1) * CB], [mT, W])
                        Xo = xo.next()
                        S.op(V, lambda: nc.vector.tensor_tensor(out=Xo[:], in0=P[:], in1=Xi[:], op=ALU.add),
                             reads=[P, Xi], writes=[Xo])
                        S.dma("sync", x1_3d[(e0 + tl) // 128, :, cb * CB:(cb + 1) * CB], Xo[:], reads=[Xo],
                              writes=[DB["x1"]])

    def stage_ffn_up():
        with Ctx(nc, S) as C:
            cvp = C.sb("cvp", [128, 4, 2 * NFB], F32)
            S.dma("sync", cvp[:], convp_d, writes=[cvp])
            HALO = 32
            NTK = HALO + 1024
            hT = C.sb("hT2", [128, 32, NTK], BF16)
            wsl = Rot([C.sb("wsl3", [128, 32 * 2 * 256], BF16) for _ in range(2)])
            raw = [Rot([C.sb("raw%d" % i, [128, NTK], F32) for _ in range(2)]) for i in range(2)]
            cv = [Rot([C.sb("cv%d" % i, [128, 1024], F32) for _ in range(2)]) for i in range(2)]
            ast = Rot([C.sb("ast", [128, 1024], BF16) for _ in range(2)])
            PS = Rot([C.ps("PS3", [128, 512], F32) for _ in range(6)])
            for sbi in range(2):
                e_first = 128 + sbi * 1024 - HALO
                load_aT(hT, h2T_d, 32, e_first, NTK, DB["h2T"])
                for jb in range(NFB // 2):
                    W = wsl.next()
                    wload(W, w_up[jb], 32 * 2 * 256)
                    for jj in range(2):
                        j = jb * 2 + jj
                        rws = []
                        for ag in range(2):
                            R = raw[ag].next()
                            rws.append(R)
                            for gi_, (t0, n) in enumerate([(0, HALO), (HALO, 512), (HALO + 512, 512)]):
                                P = PS.next()
                                mm_group(P[:, 0:n], P, 32, lambda kt: W[:, kt * 512 + ag * 256 + jj * 128:kt * 512 + ag * 256 + (jj + 1) * 128],
                                         lambda kt: hT[:, kt, t0:t0 + n], [W, hT])
                                if gi_ % 2 == 0:
                                    S.op(A, lambda: nc.scalar.copy(out=R[:, t0:t0 + n], in_=P[:, 0:n]), reads=[P],
                                         writes=[R])
                                else:
                                    S.op(V, lambda: nc.vector.tensor_copy(out=R[:, t0:t0 + n], in_=P[:, 0:n]), reads=[P],
                                         writes=[R])
                        cvs = []
                        for ag in range(2):
                            R = rws[ag]
                            Cv = cv[ag].next()
                            cvs.append(Cv)
                            col = ag * NFB + j
                            S.op(A, lambda: nc.scalar.activation(out=Cv[:], in_=R[:, HALO:NTK], func=AF.Identity,
                                                                 scale=cvp[:, 2, col:col + 1],
                                                                 bias=cvp[:, 3, col:col + 1]), reads=[R, cvp], writes=[Cv])
                            S.op(V, lambda: nc.vector.scalar_tensor_tensor(out=Cv[:], in0=R[:, HALO - 1:NTK - 1],
                                                                           scalar=cvp[:, 1, col:col + 1], in1=Cv[:],
                                                                           op0=ALU.mult, op1=ALU.add),
                                 reads=[R, cvp, Cv], writes=[Cv])
                            S.op(V, lambda: nc.vector.scalar_tensor_tensor(out=Cv[:], in0=R[:, HALO - 2:NTK - 2],
                                                                           scalar=cvp[:, 0, col:col + 1], in1=Cv[:],
                                                                           op0=ALU.mult, op1=ALU.add),
                                 reads=[R, cvp, Cv], writes=[Cv])
                        Ca, Cg = cvs
                        S.op(A, lambda: nc.scalar.activation(out=Ca[:], in_=Ca[:], func=AF.Silu), reads=[Ca], writes=[Ca])
                        O = ast.next()
                        S.op(V, lambda: nc.vector.tensor_tensor(out=O[:], in0=Ca[:], in1=Cg[:], op=ALU.mult),
                             reads=[Ca, Cg], writes=[O])
                        S.dma("sync", actT_d[j, :, sbi * 1024:(sbi + 1) * 1024], O[:], reads=[O], writes=[DB["actT"]])

    def stage_ffn_down():
        with Ctx(nc, S) as C:
            aT = C.sb("aT7", [128, NFB, 512], BF16)
            pieces = [(0, 22), (22, 22), (44, 21), (65, 21)]
            wsl = Rot([C.sb("wsl4", [128, 22 * 512], BF16) for _ in range(3)])
            xin = Rot([C.sb("xin7", [128, 512], F32) for _ in range(3)])
            xo = Rot([C.sb("xo7", [128, 512], F32) for _ in range(3)])
            junk = C.sb("junk7", [128, 512], BF16)
            PS = Rot([C.ps("PS4", [128, 512], F32) for _ in range(8)])
            for tb in range(4):
                load_aT(aT, actT_d, NFB, tb * 512, 512, DB["actT"], step=11)
                for cb in range(8):
                    Ps = [PS.next() for _ in range(4)]
                    for pi, (k0, kn) in enumerate(pieces):
                        W = wsl.next()
                        wload(W, w_down[cb, :, k0 * 512:(k0 + kn) * 512], kn * 512)
                        for tc in range(4):
                            P = Ps[tc]
                            for kk in range(kn):
                                kt = k0 + kk
                                S.op(PE, lambda: nc.tensor.matmul(P[:], lhsT=aT[:, kt, tc * 128:(tc + 1) * 128],
                                                                  rhs=W[:, kk * 512:(kk + 1) * 512], start=(kt == 0), stop=(kt == NFB - 1)),
                                     reads=[aT, W], writes=[P], sig=(kk == kn - 1))
                    for tc in range(4):
                        P = Ps[tc]
                        ch = tb * 4 + tc
                        Xi = xin.next()
                        S.dma("sync", Xi[:], x1_3d[1 + ch, :, cb * 512:(cb + 1) * 512],
                              reads=[DB["x1"]], writes=[Xi])
                        Xo = xo.next()
                        S.op(V, lambda: nc.vector.tensor_tensor(out=Xo[:], in0=P[:], in1=Xi[:], op=ALU.add),
                             reads=[P, Xi], writes=[Xo])
                        S.op(A, lambda: nc.scalar.activation(out=junk[:], in_=Xo[:], func=AF.Square,
                                                             accum_out=ssq[:, ch, cb:cb + 1]), reads=[Xo],
                             writes=[junk, ssq])
                        S.dma("sync", x2_3d[ch, :, cb * 512:(cb + 1) * 512], Xo[:], reads=[Xo],
                              writes=[DB["x2"]])
        with Ctx(nc, S) as C:
            gbc = C.sb("gbcf", [128, D], F32)
            S.dma("sync", gbc[:], g3_d[2], writes=[gbc])
            xt = Rot([C.sb("xt8", [128, D], F32) for _ in range(2)])
            yo = Rot([C.sb("yo8", [128, D], F32) for _ in range(2)])
            rs = Rot([C.sb("rs8", [128, 1], F32) for _ in range(2)])
            for ch in range(16):
                X = xt.next()
                S.dma("sync", X[:], x2_d[ch * 128:(ch + 1) * 128, :], reads=[DB["x2"]], writes=[X])
                r_ = rs.next()
                S.op(V, lambda: nc.vector.tensor_reduce(out=r_[:], in_=ssq[:, ch, :], axis=mybir.AxisListType.X,
                                                        op=ALU.add), reads=[ssq], writes=[r_])
                S.op(V, lambda: nc.vector.tensor_scalar(out=r_[:], in0=r_[:], scalar1=1.0 / D, scalar2=RMS_EPS,
                                                        op0=ALU.mult, op1=ALU.add), reads=[r_], writes=[r_])
                S.op(A, lambda: nc.scalar.activation(out=r_[:], in_=r_[:], func=AF.Sqrt), reads=[r_], writes=[r_])
                S.op(V, lambda: nc.vector.reciprocal(out=r_[:], in_=r_[:]), reads=[r_], writes=[r_])
                Y = yo.next()
                S.op(V, lambda: nc.vector.scalar_tensor_tensor(out=Y[:], in0=X[:], scalar=r_[:], in1=gbc[:],
                                                               op0=ALU.mult, op1=ALU.mult), reads=[X, r_, gbc],
                     writes=[Y])
                S.dma("sync", out_d[ch * 128:(ch + 1) * 128, :], Y[:], reads=[Y], writes=[DB["out"]])

    stages = [lambda: stage_normT(x_d, list(range(32)), 0, h1T_d, 0, DB["in"], DB["h1T"]),
              stage_inproj, stage_ssm, stage_attn, stage_mix, stage_wout,
              lambda: stage_normT(x1_d, list(range(17)), 1, h2T_d, 0, DB["x1"], DB["h2T"]),
              stage_ffn_up, stage_ffn_down]
    for si, st in enumerate(stages[:nstages]):
        if sel is None or si in sel:
            st()
    S.barrier()
    top.close()
    return nc


def _bf(x):
    return x


def host_layout(inp):
    f = np.float32
    x = np.asarray(inp["x"], f)
    sh = {}
    def tile_w(W, cbw):
        K, N = W.shape
        return np.ascontiguousarray(W.reshape(K // 128, 128, N // cbw, cbw).transpose(2, 1, 0, 3)).reshape(
            N // cbw, 128, (K // 128) * cbw)

    sh["w_in"] = tile_w(np.asarray(inp["w_in"], f)[0], 256)
    sh["w_ao"] = tile_w(np.asarray(inp["w_attn_out"], f)[0], 512)
    sh["w_out"] = tile_w(np.asarray(inp["w_out"], f)[0], 512)
    sh["w_down"] = tile_w(np.asarray(inp["w_down"], f)[0], 512)
    wg_ = np.asarray(inp["w_glu"], f)[0].reshape(16, 128, 2, 8, 512)
    sh["w_glu"] = np.ascontiguousarray(wg_.transpose(3, 1, 0, 2, 4)).reshape(8, 128, 16 * 2 * 512)
    wu_ = np.asarray(inp["w_up"], f)[0].reshape(32, 128, 2, NFB // 2, 256)
    sh["w_up"] = np.ascontiguousarray(wu_.transpose(3, 1, 0, 2, 4)).reshape(NFB // 2, 128, 32 * 2 * 256)
    g3 = np.stack([np.asarray(inp["g_mix"], f)[0], np.asarray(inp["g_ffn"], f)[0], np.asarray(inp["g_final"], f)])
    sh["g3"] = np.ascontiguousarray(np.broadcast_to(g3[:, None, :], (3, 128, D)))
    cw = np.asarray(inp["conv_w"], f)[0]
    cb = np.asarray(inp["conv_b"], f)[0]
    cp = np.stack([cw[0], cw[1], cw[2], cb])
    sh["convp"] = np.ascontiguousarray(cp.reshape(4, 2 * NFB, 128).transpose(2, 0, 1))
    ldt = np.asarray(inp["ssm_log_dt"], f)[0]
    are = np.asarray(inp["ssm_a_re"], f)[0]
    aim = np.asarray(inp["ssm_a_im"], f)[0]
    ldt_f = np.repeat(ldt, 64)
    tm = np.stack([ldt_f, are.reshape(-1), aim.reshape(-1)])
    sh["ssm_tm"] = np.ascontiguousarray(np.broadcast_to(tm[:, None, :], (3, 128, 8192)))
    def sm(a):
        return np.ascontiguousarray(a.reshape(64, 2, 64).transpose(1, 2, 0).reshape(128, 64))
    sh["ssm_sm"] = np.stack([sm(np.broadcast_to(ldt[:, None], (128, 64))), sm(are), sm(aim)]).astype(f)
    bre = np.asarray(inp["ssm_b_re"], f)[0]
    bim = np.asarray(inp["ssm_b_im"], f)[0]
    bblk = np.zeros((128, 16, 1024), f)
    for o in range(16):
        for gl in range(8):
            g = 8 * o + gl
            bblk[gl * 16:(gl + 1) * 16, o, gl * 64:(gl + 1) * 64] = bre[g].T
            bblk[gl * 16:(gl + 1) * 16, o, 512 + gl * 64:512 + (gl + 1) * 64] = bim[g].T
    sh["bblk"] = bblk
    cre = np.asarray(inp["ssm_c_re"], f)[0]
    cim = np.asarray(inp["ssm_c_im"], f)[0]
    cblk = np.zeros((2, 128, 64, 128), f)
    for pair in range(64):
        for g2 in range(2):
            g = 2 * pair + g2
            c0 = (pair % 4) * 32 + g2 * 16
            cblk[0, g2 * 64:(g2 + 1) * 64, pair, c0:c0 + 16] = cre[g].T
            cblk[1, g2 * 64:(g2 + 1) * 64, pair, c0:c0 + 16] = cim[g].T
    sh["cblk"] = cblk
    dsk = np.asarray(inp["ssm_d"], f)[0]
    sh["dsk"] = np.ascontiguousarray(dsk.reshape(16, 128).T)
    sh["sidx"] = np.arange(128, dtype=f).reshape(128, 1)
    sh["tidx"] = np.ascontiguousarray(np.broadcast_to(np.arange(128, dtype=f)[None, :], (128, 128)))
    kq = np.arange(128)
    tri = (kq[:, None] <= kq[None, :]).astype(f)
    identm = np.eye(128, dtype=f)
    maskc = np.where(kq[:, None] <= kq[None, :], 0.0, NEG).astype(f)
    maskp = np.where(kq[:, None] >= kq[None, :], 0.0, NEG).astype(f)
    prot = np.zeros((128, 128), f)
    for i in range(16):
        prot[i + 16, i] = -1.0
        prot[i, i + 16] = 1.0
    sh["ones"] = np.ones((128, 128), f)
    invf = np.zeros((128, 1), f)
    fr = (np.float32(500000.0) ** (-np.arange(0, 32, 2, dtype=f) / np.float32(32))).astype(f)
    invf[0:16, 0] = fr
    invf[16:32, 0] = fr
    sh["invf"] = invf
    for nm, bpp in W_BPP.items():
        arr = sh.pop(nm)
        for pc in range(-(-arr.shape[0] // bpp)):
            sh["%s_p%d" % (nm, pc)] = arr[pc * bpp:(pc + 1) * bpp]
    maps = []
    for c in range(NCORES):
        b, half = c // 2, c % 2
        m = dict(sh)
        xe = np.zeros((NTE, D), f)
        if half == 0:
            xe[OWN0:] = x[b, 0:2048]
        else:
            xe[:] = x[b]
        m["x"] = xe
        valid = np.full((128, 128), 1.0 if half == 1 else 0.0, f)
        m["cst"] = np.stack([tri, identm, maskp, maskc, valid, prot])
        pos = (np.arange(NTE, dtype=np.int64) + (half * 2048 - 2048)).astype(f)
        m["pos"] = np.ascontiguousarray(np.broadcast_to(pos[None, :], (128, NTE)))
        maps.append(m)
    return maps


_NC_CACHE = {}


def kernel(**inputs):
    maps = host_layout(inputs)
    if "nc" not in _NC_CACHE:
        _NC_CACHE["nc"] = build_program()
    nc = _NC_CACHE["nc"]
    res = run_bass_kernel_spmd(nc, maps, core_ids=list(range(NCORES)))
    out = np.zeros((4, SEQ, D), np.float32)
    for c in range(NCORES):
        b, half = c // 2, c % 2
        out[b, half * 2048:(half + 1) * 2048] = res.results[c]["out"]
    return out
```
